# Optimizing a Trainium2 kernel written in Bass

```python
import math
import jax, jax.numpy as jnp
from jax import lax
import numpy as np

D_MODEL = 1024
BATCH = 4
SEQ = 4096
DEPTH = 2

HEAD_DIM = 64
A_HEADS = 8
B_HEADS = 4
DILATED_CONFIGS = ((128, 1), (512, 4), (2048, 16))
MAX_WINDOW = 2048
Q_BLOCK = 128
A_WIDTH = A_HEADS * HEAD_DIM
B_QK_WIDTH = B_HEADS * 2 * HEAD_DIM
B_V_WIDTH = B_HEADS * 2 * HEAD_DIM
ATTN_IN = 3 * A_WIDTH + 2 * B_QK_WIDTH + B_V_WIDTH
ATTN_OUT = A_WIDTH + B_V_WIDTH
ATTN_SPLITS = (A_WIDTH, 2 * A_WIDTH, 3 * A_WIDTH, 3 * A_WIDTH + B_QK_WIDTH, 3 * A_WIDTH + 2 * B_QK_WIDTH)
LRU_WIDTH = D_MODEL
LRU_BLOCKS = 8
LRU_BLOCK_WIDTH = LRU_WIDTH // LRU_BLOCKS
LRU_C = 8.0
REC_CONV = 4
FFN_DIM = 3 * D_MODEL
FFN_CONV = 3

N_EVEN = (DEPTH + 1) // 2
N_ODD = DEPTH // 2
NORM_EPS = 1e-6
NEG_INF = -1e30

kernel_name = "hybrid_dilated_diff_rglru_convffn"


def rms_norm(x, g):
    xf = x.astype(jnp.float32)
    y = xf * lax.rsqrt(jnp.mean(xf * xf, axis=-1, keepdims=True) + NORM_EPS)
    return (y * g.astype(jnp.float32)).astype(x.dtype)


def alibi_slopes(n):
    return jnp.exp2(-8.0 * jnp.arange(1, n + 1, dtype=jnp.float32) / n)


def causal_depthwise_conv(x, w, b):
    K, C = w.shape
    y = lax.conv_general_dilated(
        x, w.astype(x.dtype)[:, None, :], window_strides=(1,), padding=((K - 1, 0),),
        dimension_numbers=('NWC', 'WIO', 'NWC'), feature_group_count=C)
    return y + b.astype(x.dtype)


def dilated_branch(q, k, v, slopes, window, dilation):
    B, S, H, E = q.shape
    n = window // dilation
    L = S // dilation
    nb = L // n

    def to_blocks(t):
        t = t.reshape(B, L, dilation, H, E).transpose(0, 2, 3, 1, 4)
        return t.reshape(B, dilation, H, nb, n, E)

    def with_prev(t):
        prev = jnp.pad(t[:, :, :, :-1], ((0, 0), (0, 0), (0, 0), (1, 0), (0, 0), (0, 0)))
        return jnp.concatenate([prev, t], axis=4)

    qb = to_blocks(q)
    kb = with_prev(to_blocks(k))
    vb = with_prev(to_blocks(v))
    s = jnp.einsum('bdhnqe,bdhnke->bdhnqk', qb, kb) * (E ** -0.5)
    kj = jnp.arange(2 * n)[None, :]
    delta = jnp.arange(n)[:, None] + n - kj
    in_band = (delta >= 0) & (delta <= n)
    after_start = (jnp.arange(nb)[:, None, None] > 0) | (kj[None] >= n)
    valid = in_band[None] & after_start
    dist = (delta * dilation).astype(jnp.float32)
    s = jnp.where(valid, s - slopes[:, None, None, None] * dist, NEG_INF)
    m = jnp.max(s, axis=-1, keepdims=True)
    e = jnp.exp(s - m)
    den = jnp.sum(e, axis=-1)
    o = jnp.einsum('bdhnqk,bdhnke->bdhnqe', e, vb) / den[..., None]
    lse = m[..., 0] + jnp.log(den)
    o = o.reshape(B, dilation, H, L, E).transpose(0, 3, 1, 2, 4).reshape(B, S, H, E)
    lse = lse.reshape(B, dilation, H, L).transpose(0, 3, 1, 2).reshape(B, S, H)
    return o, lse


def dilated_mixture_attention(q, k, v, slopes):
    B, S, H, E = q.shape
    s_pad = -(-S // MAX_WINDOW) * MAX_WINDOW
    pad = ((0, 0), (0, s_pad - S), (0, 0), (0, 0))
    qf, kf, vf = [jnp.pad(t.astype(jnp.float32), pad) for t in (q, k, v)]
    outs, lses = zip(*[dilated_branch(qf, kf, vf, slopes, w, d) for w, d in DILATED_CONFIGS])
    weights = jax.nn.softmax(jnp.stack(lses), axis=0)
    o = jnp.sum(weights[..., None] * jnp.stack(outs), axis=0)
    return o[:, :S]


def differential_attention(q, k, v, slopes, lam):
    B, S, H, _, E = q.shape
    n_blocks = S // Q_BLOCK
    kf = k.astype(jnp.float32)
    vf = v.astype(jnp.float32)
    kpos = jnp.arange(S)
    sl = slopes[:, None, None, None]

    def one_block(i):
        qs = lax.dynamic_slice_in_dim(q, i * Q_BLOCK, Q_BLOCK, axis=1).astype(jnp.float32)
        s = jnp.einsum('bqhce,bkhce->bhcqk', qs, kf) * (E ** -0.5)
        dist = (i * Q_BLOCK + jnp.arange(Q_BLOCK))[:, None] - kpos[None, :]
        s = jnp.where(dist >= 0, s - sl * dist.astype(jnp.float32), NEG_INF)
        p = jax.nn.softmax(s, axis=-1)
        attn = p[:, :, 0] - lam * p[:, :, 1]
        return jnp.einsum('bhqk,bkhe->bqhe', attn, vf)

    o = lax.map(one_block, jnp.arange(n_blocks))
    return jnp.moveaxis(o, 0, 1).reshape(B, S, H, 2 * E)


def hybrid_attention_mixer(x, w_in, w_out, a_q_norm, a_k_norm, b_q_norm, b_k_norm, b_sub_norm,
                           lam_q1, lam_k1, lam_q2, lam_k2, lam_init):
    B, S, _ = x.shape
    proj = x @ w_in.astype(x.dtype)
    qa, ka, va, qb, kb, vb = jnp.split(proj, list(ATTN_SPLITS), axis=-1)
    qa = rms_norm(qa.reshape(B, S, A_HEADS, HEAD_DIM), a_q_norm)
    ka = rms_norm(ka.reshape(B, S, A_HEADS, HEAD_DIM), a_k_norm)
    va = va.reshape(B, S, A_HEADS, HEAD_DIM)
    qb = rms_norm(qb.reshape(B, S, B_HEADS, 2, HEAD_DIM), b_q_norm)
    kb = rms_norm(kb.reshape(B, S, B_HEADS, 2, HEAD_DIM), b_k_norm)
    vb = vb.reshape(B, S, B_HEADS, 2 * HEAD_DIM)
    slopes = alibi_slopes(A_HEADS + B_HEADS)
    oa = dilated_mixture_attention(qa, ka, va, slopes[:A_HEADS])
    f32 = jnp.float32
    lam = (jnp.exp(jnp.sum(lam_q1.astype(f32) * lam_k1.astype(f32)))
           - jnp.exp(jnp.sum(lam_q2.astype(f32) * lam_k2.astype(f32))) + lam_init)
    ob = differential_attention(qb, kb, vb, slopes[A_HEADS:], lam)
    ob = rms_norm(ob, b_sub_norm) * (1.0 - lam_init)
    y = jnp.concatenate([oa.reshape(B, S, A_WIDTH), ob.reshape(B, S, B_V_WIDTH)], axis=-1).astype(x.dtype)
    return y @ w_out.astype(x.dtype)


def linear_combine(left, right):
    a_l, b_l = left
    a_r, b_r = right
    return a_l * a_r, a_r * b_l + b_r


def rg_lru(x, gate_a_w, gate_a_b, gate_x_w, gate_x_b, a_param):
    B, S, C = x.shape
    xb = x.reshape(B, S, LRU_BLOCKS, LRU_BLOCK_WIDTH)

    def block_diag(w, b):
        y = jnp.einsum('bsnc,ncd->bsnd', xb, w.astype(x.dtype)).reshape(B, S, C) + b.astype(x.dtype)
        return y.astype(jnp.float32)

    r = jax.nn.sigmoid(block_diag(gate_a_w, gate_a_b))
    i = jax.nn.sigmoid(block_diag(gate_x_w, gate_x_b))
    log_a = -LRU_C * r * jax.nn.softplus(-a_param.astype(jnp.float32))
    a = jnp.exp(log_a)
    u = jnp.sqrt(-jnp.expm1(2.0 * log_a)) * (i * x.astype(jnp.float32))
    _, h = lax.associative_scan(linear_combine, (a, u), axis=1)
    return h.astype(x.dtype)


def recurrent_mixer(x, w_in, conv_w, conv_b, gate_a_w, gate_a_b, gate_x_w, gate_x_b, a_param, w_out):
    gate, xr = jnp.split(x @ w_in.astype(x.dtype), [LRU_WIDTH], axis=-1)
    xr = causal_depthwise_conv(xr, conv_w, conv_b)
    h = rg_lru(xr, gate_a_w, gate_a_b, gate_x_w, gate_x_b, a_param)
    return (h * jax.nn.gelu(gate)) @ w_out.astype(x.dtype)


def conv_ffn(x, w_up, conv_w, conv_b, w_down):
    u = causal_depthwise_conv(x @ w_up.astype(x.dtype), conv_w, conv_b)
    g, val = jnp.split(u, [FFN_DIM], axis=-1)
    return (jax.nn.gelu(g) * val) @ w_down.astype(x.dtype)


def setup_inputs(seed: int = 0) -> dict:
    key = jax.random.key(seed)
    ks = iter(jax.random.split(key, 40))
    f32 = jnp.float32

    def normal(shape, scale):
        return scale * jax.random.normal(next(ks), shape, f32)

    def gain(shape):
        return 1.0 + 0.02 * jax.random.normal(next(ks), shape, f32)

    u = jax.random.uniform(next(ks), (N_ODD, LRU_WIDTH), f32, 0.9, 0.999)
    a0 = u ** (1.0 / LRU_C)
    a_param = jnp.log(a0) - jnp.log1p(-a0)
    return {
        'x': normal((BATCH, SEQ, D_MODEL), 1.0),
        'attn_norm': gain((N_EVEN, D_MODEL)),
        'attn_w_in': normal((N_EVEN, D_MODEL, ATTN_IN), D_MODEL ** -0.5),
        'attn_w_out': normal((N_EVEN, ATTN_OUT, D_MODEL), ATTN_OUT ** -0.5),
        'a_q_norm': gain((N_EVEN, HEAD_DIM)),
        'a_k_norm': gain((N_EVEN, HEAD_DIM)),
        'b_q_norm': gain((N_EVEN, HEAD_DIM)),
        'b_k_norm': gain((N_EVEN, HEAD_DIM)),
        'b_sub_norm': gain((N_EVEN, 2 * HEAD_DIM)),
        'b_lam_q1': normal((N_EVEN, HEAD_DIM), 0.1),
        'b_lam_k1': normal((N_EVEN, HEAD_DIM), 0.1),
        'b_lam_q2': normal((N_EVEN, HEAD_DIM), 0.1),
        'b_lam_k2': normal((N_EVEN, HEAD_DIM), 0.1),
        'rec_norm': gain((N_ODD, D_MODEL)),
        'rec_w_in': normal((N_ODD, D_MODEL, 2 * LRU_WIDTH), D_MODEL ** -0.5),
        'rec_conv_w': normal((N_ODD, REC_CONV, LRU_WIDTH), REC_CONV ** -0.5),
        'rec_conv_b': normal((N_ODD, LRU_WIDTH), 0.01),
        'rec_gate_a_w': normal((N_ODD, LRU_BLOCKS, LRU_BLOCK_WIDTH, LRU_BLOCK_WIDTH), LRU_BLOCK_WIDTH ** -0.5),
        'rec_gate_a_b': normal((N_ODD, LRU_WIDTH), 0.01),
        'rec_gate_x_w': normal((N_ODD, LRU_BLOCKS, LRU_BLOCK_WIDTH, LRU_BLOCK_WIDTH), LRU_BLOCK_WIDTH ** -0.5),
        'rec_gate_x_b': normal((N_ODD, LRU_WIDTH), 0.01),
        'rec_a_param': a_param,
        'rec_w_out': normal((N_ODD, LRU_WIDTH, D_MODEL), LRU_WIDTH ** -0.5),
        'ffn_norm': gain((DEPTH, D_MODEL)),
        'ffn_w_up': normal((DEPTH, D_MODEL, 2 * FFN_DIM), D_MODEL ** -0.5),
        'ffn_conv_w': normal((DEPTH, FFN_CONV, 2 * FFN_DIM), FFN_CONV ** -0.5),
        'ffn_conv_b': normal((DEPTH, 2 * FFN_DIM), 0.01),
        'ffn_w_down': normal((DEPTH, FFN_DIM, D_MODEL), FFN_DIM ** -0.5),
    }


def reference(x, attn_norm, attn_w_in, attn_w_out, a_q_norm, a_k_norm, b_q_norm, b_k_norm, b_sub_norm,
              b_lam_q1, b_lam_k1, b_lam_q2, b_lam_k2, rec_norm, rec_w_in, rec_conv_w, rec_conv_b,
              rec_gate_a_w, rec_gate_a_b, rec_gate_x_w, rec_gate_x_b, rec_a_param, rec_w_out,
              ffn_norm, ffn_w_up, ffn_conv_w, ffn_conv_b, ffn_w_down):
    h = x
    for layer in range(DEPTH):
        j = layer // 2
        if layer % 2 == 0:
            lam_init = 0.8 - 0.6 * math.exp(-0.3 * layer)
            h = h + hybrid_attention_mixer(
                rms_norm(h, attn_norm[j]), attn_w_in[j], attn_w_out[j], a_q_norm[j], a_k_norm[j],
                b_q_norm[j], b_k_norm[j], b_sub_norm[j], b_lam_q1[j], b_lam_k1[j], b_lam_q2[j], b_lam_k2[j],
                lam_init)
        else:
            h = h + recurrent_mixer(
                rms_norm(h, rec_norm[j]), rec_w_in[j], rec_conv_w[j], rec_conv_b[j], rec_gate_a_w[j],
                rec_gate_a_b[j], rec_gate_x_w[j], rec_gate_x_b[j], rec_a_param[j], rec_w_out[j])
        h = h + conv_ffn(rms_norm(h, ffn_norm[layer]), ffn_w_up[layer], ffn_conv_w[layer],
                         ffn_conv_b[layer], ffn_w_down[layer])
    return h
```

```python
import math
import numpy as np
import concourse.bass as bass
import concourse.mybir as mybir
from concourse.bass_utils import run_bass_kernel_spmd

F32 = mybir.dt.float32
BF16 = mybir.dt.bfloat16
AF = mybir.ActivationFunctionType
ALU = mybir.AluOpType
ENGS = ("pe", "act", "dve", "pool", "sp")

D = 1024
T = 2048
TG = 512
NTG = 4
NCH = 8
HALO = 4
FFN = 3072
NEG = -30000.0
EPS = 1e-6
SLOPES = [2.0 ** (-8.0 * (i + 1) / 12.0) for i in range(12)]
DIL = ((128, 1), (512, 4), (2048, 16))


class Buf:
    def __init__(self, t, space, nparts, ncols, esize, off=0):
        self.t, self.space, self.nparts, self.ncols, self.esize, self.off = t, space, nparts, ncols, esize, off

    def reg(self, c0=0, c1=None, p0=0, p1=None):
        if c1 is None:
            c1 = self.ncols
        if p1 is None:
            p1 = self.nparts
        if self.space[0] == "p":
            return (self.space, 0, 128, 0, 2048)
        return (self.space, p0, p1, self.off + c0 * self.esize, self.off + c1 * self.esize)


class V:
    def __init__(self, buf, c0=0, c1=None, p0=0, p1=None, step=1):
        self.buf = buf
        self.c0 = c0
        self.c1 = buf.ncols if c1 is None else c1
        self.p0 = p0
        self.p1 = buf.nparts if p1 is None else p1
        self.step = step

    def ap(self):
        if self.step == 1:
            return self.buf.t[self.p0:self.p1, self.c0:self.c1]
        return self.buf.t[self.p0:self.p1, self.c0:self.c1:self.step]

    def reg(self):
        return self.buf.reg(self.c0, self.c1, self.p0, self.p1)


def _overlap(a, b):
    return a[0] == b[0] and a[1] < b[2] and b[1] < a[2] and a[3] < b[4] and b[3] < a[4]


def _covers(a, b):
    return a[0] == b[0] and a[1] <= b[1] and a[2] >= b[2] and a[3] <= b[3] and a[4] >= b[4]


class Op:
    __slots__ = ("eng", "idx", "fn", "deps", "dma_key", "dma_val", "signal", "clock", "gseq")


class Sched:
    def __init__(self, nc, arena_bytes):
        self.nc = nc
        self.ops = {e: [] for e in ENGS}
        self.acc = {}
        self.dma_keys = {}
        self.sems = {e: nc.alloc_semaphore("s_" + e) for e in ENGS}
        self._ctr = 0
        self._gseq = 0
        self.arena = nc.alloc_sbuf_tensor("arena", [128, arena_bytes // 4], F32)
        self.arena_bytes = arena_bytes
        self.sp_ = 0
        self.hiwater = 0

    def alloc(self, name, ncols, dtype):
        es = 2 if dtype == BF16 else 4
        nbytes = (ncols * es + 63) // 64 * 64
        off = self.sp_
        self.sp_ += nbytes
        self.hiwater = max(self.hiwater, self.sp_)
        assert self.sp_ <= self.arena_bytes, "SBUF arena overflow %s: %d > %d" % (name, self.sp_, self.arena_bytes)
        ap = self.arena[:, off // 4:(off + nbytes) // 4]
        if dtype == BF16:
            ap = ap.bitcast(BF16)
        ap = ap[:, 0:ncols]
        return Buf(ap, "sb", 128, ncols, es, off)

    def mark(self):
        return self.sp_

    def release(self, m):
        self.sp_ = m

    def psum(self, name, ncols=512, dtype=F32):
        t = self.nc.alloc_psum_tensor(name, [128, ncols], dtype)
        self._ctr += 1
        return Buf(t, "ps%d" % self._ctr, 128, ncols, 4 if dtype == F32 else 2)

    def dram(self, name, shape, dtype, kind):
        t = self.nc.dram_tensor(name, list(shape), dtype, kind=kind)
        self._ctr += 1
        es = 2 if dtype == BF16 else 4
        ncols = int(np.prod(shape[1:])) if len(shape) > 1 else 1
        return Buf(t, "dr%d" % self._ctr, shape[0], ncols, es)

    def op(self, eng, fn, reads=(), writes=(), dma_key=None):
        o = Op()
        o.eng, o.fn = eng, fn
        o.idx = len(self.ops[eng])
        self._gseq += 1
        o.gseq = self._gseq
        o.signal = False
        o.dma_key = dma_key
        o.dma_val = None
        if dma_key is not None:
            if dma_key not in self.dma_keys:
                self.dma_keys[dma_key] = [self.nc.alloc_semaphore("d_%d" % len(self.dma_keys)), 0]
            self.dma_keys[dma_key][1] += 16
            o.dma_val = self.dma_keys[dma_key][1]
        best = {}

        def consider(d):
            k = ("dma", d.dma_key) if d.dma_key is not None else ("eng", d.eng)
            v = d.dma_val if d.dma_key is not None else d.idx
            if k not in best or v > best[k][0]:
                best[k] = (v, d)

        for r in reads:
            for (reg, op_, w) in self.acc.get(r[0], ()):
                if w and _overlap(reg, r):
                    consider(op_)
        for wr in writes:
            for (reg, op_, w) in self.acc.get(wr[0], ()):
                if _overlap(reg, wr):
                    consider(op_)
        o.deps = [x[1] for x in best.values()]
        self.ops[eng].append(o)
        for r in reads:
            self.acc.setdefault(r[0], []).append((r, o, False))
        for wr in writes:
            lst = self.acc.setdefault(wr[0], [])
            lst[:] = [x for x in lst if not _covers(wr, x[0])]
            lst.append((wr, o, True))
        return o

    def emit(self, final_waits=()):
        nc = self.nc
        eidx = {e: i for i, e in enumerate(ENGS)}
        needed = {}
        cur = {e: [-1] * len(ENGS) for e in ENGS}
        dma_waited = {e: {} for e in ENGS}
        allops = []
        for e in ENGS:
            allops.extend(self.ops[e])
        allops.sort(key=lambda o: o.gseq)
        nwaits = 0
        for o in allops:
            e = o.eng
            clk = cur[e]
            waits = []
            for d in o.deps:
                if d.dma_key is not None:
                    if dma_waited[e].get(d.dma_key, 0) < d.dma_val:
                        dma_waited[e][d.dma_key] = d.dma_val
                        waits.append(("dma", d.dma_key, d.dma_val))
                    dc = d.clock
                    for i in range(len(ENGS)):
                        if dc[i] > clk[i]:
                            clk[i] = dc[i]
                    continue
                fi = eidx[d.eng]
                if d.eng == e and e == "pe":
                    continue
                if clk[fi] >= d.idx:
                    continue
                d.signal = True
                waits.append(("eng", d.eng, d))
                dc = d.clock
                for i in range(len(ENGS)):
                    if dc[i] > clk[i]:
                        clk[i] = dc[i]
                clk[fi] = max(clk[fi], d.idx)
            needed[o] = waits
            nwaits += len(waits)
            o.clock = list(clk)
        cnt = {}
        for e in ENGS:
            c = 0
            for o in self.ops[e]:
                if o.signal:
                    c += 1
                cnt[o] = c
        self.stats = {e: len(self.ops[e]) for e in ENGS}
        self.stats["waits"] = nwaits

        def run_engine(e, engobj):
            for o in self.ops[e]:
                for w in needed[o]:
                    if w[0] == "dma":
                        engobj.wait_ge(self.dma_keys[w[1]][0], w[2])
                    else:
                        engobj.wait_ge(self.sems[w[1]], cnt[w[2]])
                ins = o.fn(engobj)
                if o.dma_key is not None:
                    ins.then_inc(self.dma_keys[o.dma_key][0], 16)
                elif o.signal:
                    ins.then_inc(self.sems[e], 1)
            if e == "sp":
                for k in final_waits:
                    engobj.wait_ge(self.dma_keys[k][0], self.dma_keys[k][1])

        with nc.Block() as block:
            @block.tensor
            def _(eng):
                run_engine("pe", eng)

            @block.scalar
            def _(eng):
                run_engine("act", eng)

            @block.vector
            def _(eng):
                run_engine("dve", eng)

            @block.gpsimd
            def _(eng):
                run_engine("pool", eng)

            @block.sync
            def _(eng):
                run_engine("sp", eng)


def _sc(x):
    return x.ap() if isinstance(x, V) else x


def _rd(*xs):
    return [x.reg() for x in xs if isinstance(x, V)]


def mm(S, out, pairs, start=True, stop=True):
    pairs = list(pairs)

    def fn(e):
        n = len(pairs)
        ins = None
        for i, (l, r) in enumerate(pairs):
            ins = e.matmul(out.ap(), lhsT=l.ap(), rhs=r.ap(), start=(start and i == 0), stop=(stop and i == n - 1))
        return ins
    rs = []
    for l, r in pairs:
        rs.append(l.reg())
        rs.append(r.reg())
    return S.op("pe", fn, reads=rs, writes=[out.reg()])


import os as _os
_PRELOAD = _os.environ.get("PRELOAD", "0") == "1"


def act(S, out, in_, func, bias=0.0, scale=1.0):
    def fn(e):
        if _PRELOAD and func in (AF.Gelu_apprx_tanh, AF.Tanh):
            e.preload_act_table(func)
        return e.activation(out=out.ap(), in_=in_.ap(), func=func, bias=_sc(bias), scale=_sc(scale))
    return S.op("act", fn, reads=_rd(in_, bias, scale), writes=[out.reg()])


def ts(S, out, in0, s1, s2, op0, op1=None, eng="dve"):
    if op1 is None:
        return S.op(eng, lambda e: e.tensor_scalar(out.ap(), in0.ap(), _sc(s1), None, op0),
                    reads=_rd(in0, s1), writes=[out.reg()])
    return S.op(eng, lambda e: e.tensor_scalar(out.ap(), in0.ap(), _sc(s1), _sc(s2), op0, op1),
                reads=_rd(in0, s1, s2), writes=[out.reg()])


def stt(S, out, in0, s, in1, op0, op1):
    return S.op("dve", lambda e: e.scalar_tensor_tensor(out.ap(), in0.ap(), _sc(s), in1.ap(), op0, op1),
                reads=_rd(in0, s, in1), writes=[out.reg()])


def tt(S, out, in0, in1, op, eng="dve"):
    return S.op(eng, lambda e: e.tensor_tensor(out.ap(), in0.ap(), in1.ap(), op),
                reads=_rd(in0, in1), writes=[out.reg()])


def cp(S, out, in_, eng="dve"):
    if eng == "act":
        return act(S, out, in_, AF.Identity)
    return S.op(eng, lambda e: e.tensor_copy(out.ap(), in_.ap()), reads=_rd(in_), writes=[out.reg()])


def memset(S, out, val, eng="pool"):
    return S.op(eng, lambda e: e.memset(out.ap(), val), writes=[out.reg()])


def dma(S, q, out_ap, in_ap, key, reads=(), writes=()):
    if q == "pool":
        return S.op(q, lambda e: e.dma_start(out=out_ap, in_=in_ap, max_dma_last_dim=2048), reads=list(reads), writes=list(writes), dma_key=key)
    return S.op(q, lambda e: e.dma_start(out=out_ap, in_=in_ap), reads=list(reads), writes=list(writes), dma_key=key)


def load_w(S, wd, r0, nk, c0, ncols, dst, key, q="pool"):
    src = wd.t.ap()[r0:r0 + nk * 128, c0:c0 + ncols].rearrange("(kc p) n -> p kc n", p=128)
    d = dst.t[:, 0:nk * ncols].rearrange("p (kc n) -> p kc n", kc=nk)
    return dma(S, q, d, src, key, writes=[dst.reg(0, nk * ncols)])
def _vec_layout():
    VEC = {}
    n = [0]

    def add(name, w):
        VEC[name] = n[0]
        n[0] += w
    for nm in ("attn_norm", "ffn_norm0", "rec_norm", "ffn_norm1"):
        add(nm, 8)
    for L in range(2):
        for k in range(3):
            add("ffn_cw%d_%d" % (L, k), 48)
        add("ffn_cb%d" % L, 48)
    for k in range(4):
        add("rec_cw_%d" % k, 8)
    for nm in ("rec_cb", "ga_b", "gx_b", "a_param"):
        add(nm, 8)
    for nm in ("aq", "ak", "bq", "bk", "bsub", "lq1", "lk1", "lq2", "lk2", "padneg", "hsel"):
        add(nm, 1)
    return VEC, n[0]


VEC, NV = _vec_layout()


def _cols(v):
    v = np.asarray(v, np.float32).reshape(-1, 128)
    return np.ascontiguousarray(v.T)


def build_vecs(inp, half):
    out = np.zeros((128, NV), np.float32)

    def put(name, arr):
        out[:, VEC[name]:VEC[name] + arr.shape[1]] = arr
    put("attn_norm", _cols(inp["attn_norm"][0]))
    put("ffn_norm0", _cols(inp["ffn_norm"][0]))
    put("rec_norm", _cols(inp["rec_norm"][0]))
    put("ffn_norm1", _cols(inp["ffn_norm"][1]))
    for L in range(2):
        for k in range(3):
            put("ffn_cw%d_%d" % (L, k), _cols(inp["ffn_conv_w"][L, k]))
        put("ffn_cb%d" % L, _cols(inp["ffn_conv_b"][L]))
    for k in range(4):
        put("rec_cw_%d" % k, _cols(inp["rec_conv_w"][0, k]))
    put("rec_cb", _cols(inp["rec_conv_b"][0]))
    put("ga_b", _cols(inp["rec_gate_a_b"][0]))
    put("gx_b", _cols(inp["rec_gate_x_b"][0]))
    put("a_param", _cols(inp["rec_a_param"][0]))
    for nm, key in (("aq", "a_q_norm"), ("ak", "a_k_norm"), ("bq", "b_q_norm"), ("bk", "b_k_norm")):
        put(nm, np.tile(np.asarray(inp[key][0], np.float32), 2)[:, None])
    put("bsub", np.asarray(inp["b_sub_norm"][0], np.float32)[:, None])
    for nm, key in (("lq1", "b_lam_q1"), ("lk1", "b_lam_k1"), ("lq2", "b_lam_q2"), ("lk2", "b_lam_k2")):
        col = np.zeros((128, 1), np.float32)
        col[:64, 0] = np.asarray(inp[key][0], np.float32)
        put(nm, col)
    put("padneg", np.full((128, 1), 0.0 if half == 1 else NEG, np.float32))
    put("hsel", np.full((128, 1), 1.0 if half == 1 else 0.0, np.float32))
    return out


def lay_wup(w):
    w = np.asarray(w, np.float32)
    g = w[:, :FFN].reshape(8, 128, 24, 128)
    v = w[:, FFN:].reshape(8, 128, 24, 128)
    gv = np.concatenate([g, v], axis=3)
    return np.ascontiguousarray(gv.transpose(2, 1, 0, 3).reshape(24, 128, 8 * 256))


def lay_wdn(w, G=4):
    w = np.asarray(w, np.float32).reshape(24 // G, G, 128, 1024)
    return np.ascontiguousarray(w.transpose(0, 2, 1, 3).reshape(24 // G, 128, G * 1024))


def lay_kmajor(w, ncols):
    w = np.asarray(w, np.float32)
    K, N = w.shape
    x = w.reshape(K // 128, 128, N // ncols, ncols)
    return np.ascontiguousarray(x.transpose(2, 1, 0, 3).reshape(N // ncols, 128, (K // 128) * ncols))


def lay_T(x):
    x = np.asarray(x, np.float32)
    t = x.shape[0]
    return np.ascontiguousarray(x.reshape(t, 8, 128).transpose(2, 1, 0).reshape(128, 8 * t))


def unlay_T(y, t):
    return np.ascontiguousarray(y.reshape(128, 8, t).transpose(2, 1, 0).reshape(t, 1024))


def lay_win(w):
    w = np.asarray(w, np.float32)
    g = w[:, :D].reshape(8, 128, 8, 128)
    x = w[:, D:].reshape(8, 128, 8, 128)
    gx = np.concatenate([g, x], axis=3)
    return np.ascontiguousarray(gx.transpose(2, 1, 0, 3).reshape(8, 128, 8 * 256))


def lay_wax(wa, wx):
    return np.ascontiguousarray(np.concatenate([np.asarray(wa, np.float32), np.asarray(wx, np.float32)], axis=2))


ACST = {}
def _acst_layout():
    n = 0
    for nm, w in (("ident", 128), ("bd64", 128), ("ddist", 256), ("dmask", 256), ("tri8", 128), ("dbias", 4 * 2 * 36)):
        ACST[nm] = n
        n += w
    return n
ACST_N = _acst_layout()


def build_attn_consts(half):
    c = np.zeros((128, ACST_N), np.float32)
    k = np.arange(128)[:, None].astype(np.float32)
    q = np.arange(128)[None, :].astype(np.float32)
    c[:, ACST["ident"]:ACST["ident"] + 128] = np.eye(128, dtype=np.float32)
    bd = np.zeros((128, 128), np.float32); bd[:64, :64] = 1; bd[64:, 64:] = 1
    c[:, ACST["bd64"]:ACST["bd64"] + 128] = bd
    dprev = 128 + q - k
    down = q - k
    c[:, ACST["ddist"]:ACST["ddist"] + 128] = np.where(k >= q, dprev, 0)
    c[:, ACST["ddist"] + 128:ACST["ddist"] + 256] = np.where(k <= q, down, 0)
    c[:, ACST["dmask"]:ACST["dmask"] + 128] = np.where(k >= q, 0, 8 * NEG)
    c[:, ACST["dmask"] + 128:ACST["dmask"] + 256] = np.where(k <= q, 0, 8 * NEG)
    c[:, ACST["tri8"]:ACST["tri8"] + 128] = np.where(k <= q, 0, 8 * NEG)
    for h in range(4):
        sl = SLOPES[8 + h]
        for pad in range(2):
            for idx in range(36):
                col = ACST["dbias"] + (h * 2 + pad) * 36 + idx
                v = sl * ((idx - 28) * 128 + k[:, 0])
                if pad and half == 0:
                    v = v + NEG
                c[:, col] = v
    return c


def lay_attn_win(w):
    w = np.asarray(w, np.float32)
    out = np.zeros((8, 128, 8, 384), np.float32)
    for g in range(8):
        if g < 4:
            cq, ck, cv = g * 128, 512 + g * 128, 1024 + g * 128
        else:
            hh = g - 4
            cq, ck, cv = 1536 + hh * 128, 2048 + hh * 128, 2560 + hh * 128
        for i, c0 in enumerate((cq, ck, cv)):
            out[g, :, :, i * 128:(i + 1) * 128] = w[:, c0:c0 + 128].reshape(8, 128, 128).transpose(1, 0, 2)
    return np.ascontiguousarray(out.reshape(8, 128, 8 * 384))
import os
def rmsnorm(S, C, src, dst, gcol, groups, nfeat_inv=1.0 / D):
    for (c0, n) in groups:
        st = V(C.PS[4], 0, n)
        for c in range(NCH):
            sq = V(C.sqb[c % 3], 0, n)
            act(S, sq, src(c, c0, n), AF.Square)
            mm(S, st, [(V(C.ones_bf), sq)], start=(c == 0), stop=(c == NCH - 1))
        lnt = V(C.lnt, 0, n)
        act(S, lnt, st, AF.Ln, bias=EPS, scale=nfeat_inv)
        rs = V(C.PS[5], 0, n)
        act(S, rs, lnt, AF.Exp, scale=-0.5)
        for c in range(NCH):
            stt(S, dst(c, c0, n), src(c, c0, n), V(C.vecs, gcol + c, gcol + c + 1), rs, ALU.mult, ALU.mult)


def conv_taps(S, C, P, acc, tail_prev, tail_new, wcols, bcol, K, n=TG):
    vw = lambda k: V(C.vecs, wcols[k], wcols[k] + 1)
    act(S, V(acc, 0, n), V(P.buf, P.c0, P.c0 + n), AF.Identity, bias=V(C.vecs, bcol, bcol + 1), scale=vw(K - 1))
    thunks = []
    for s in range(1, K):
        k = K - 1 - s
        thunks.append(lambda s=s, k=k: stt(S, V(acc, s, n), V(P.buf, P.c0, P.c0 + n - s), vw(k), V(acc, s, n), ALU.mult, ALU.add))
        thunks.append(lambda s=s, k=k: stt(S, V(acc, 0, s), V(tail_prev, K - 1 - s, K - 1), vw(k), V(acc, 0, s), ALU.mult, ALU.add))
    if tail_new is not None:
        thunks.append(lambda: cp(S, V(tail_new, 0, K - 1), V(P.buf, P.c0 + n - (K - 1), P.c0 + n), eng="dve"))
    return thunks


def ffn_phase(S, C, L, wup, wdn):
    XW = HALO + T
    nh = 2
    m0 = S.mark()
    XN = S.alloc("ffn_xn", NCH * XW, BF16)
    G = 4
    NG = (FFN // 128) // G
    abuf = [S.alloc("ffn_a%d" % i, G * T, BF16) for i in range(2)]
    wdb = [S.alloc("ffn_wd%d" % i, G * D, BF16) for i in range(2)]
    wgv = [S.alloc("ffn_wgv%d" % i, NCH * 256, BF16) for i in range(3)]
    accg = [S.alloc("ffn_accg%d" % i, TG, F32) for i in range(2)]
    accv = [S.alloc("ffn_accv%d" % i, TG, F32) for i in range(2)]
    gel = [S.alloc("ffn_gel%d" % i, TG, F32) for i in range(4)]
    tails = [[S.alloc("ffn_tl%d%d" % (h, i), 2, F32) for i in range(2)] for h in range(2)]
    gname = "ffn_norm%d" % L
    src = lambda c, c0, n: (V(C.HH, c * HALO + c0 + HALO, c * HALO + c0 + HALO + n) if c0 < 0
                            else V(C.H, c * T + c0, c * T + c0 + n))
    dst = lambda c, c0, n: V(XN, c * XW + HALO + c0, c * XW + HALO + c0 + n)
    import os
    if int(os.environ.get('FFN_LVL', '9')) >= 0 and 'N' not in os.environ.get('SKIP', ''):
        rmsnorm(S, C, src, dst, C.VEC[gname], [(-nh, nh)] + [(tg * TG, TG) for tg in range(NTG)])
    cw = [C.VEC["ffn_cw%d_%d" % (L, k)] for k in range(3)]
    cb = C.VEC["ffn_cb%d" % L]
    ucount = [0]

    def up_chunk(i, j, ab):
        slot = i % 3
        W = wgv[slot]
        dma(S, "pool", W.t[:, :], wup.t.ap()[i], ("ffn_wgv", slot), writes=[W.reg()])
        SK = os.environ.get("SKIP", "")
        for h in range(2):
            if "h" in SK:
                memset(S, V(tails[h][0]), 0.0)
                continue
            ph = V(C.PS[4 + h], 0, nh)
            mm(S, ph, [(V(W, k * 256 + h * 128, k * 256 + h * 128 + 128), V(XN, k * XW + HALO - nh, k * XW + HALO)) for k in range(NCH)])
            cp(S, V(tails[h][0], 0, nh), ph, eng="dve")
        for tg in range(NTG):
            u = ucount[0]
            ucount[0] += 1
            Pg = V(C.PS[0 + (u % 2)], 0, TG)
            Pv = V(C.PS[2 + (u % 2)], 0, TG)
            for h, P in ((0, Pg), (1, Pv)):
                mm(S, P, [(V(W, k * 256 + h * 128, k * 256 + h * 128 + 128), V(XN, k * XW + HALO + tg * TG, k * XW + HALO + (tg + 1) * TG)) for k in range(NCH)])
            ag, av = accg[u % 2], accv[u % 2]
            tg_th = conv_taps(S, C, Pg, ag, tails[0][tg % 2], tails[0][(tg + 1) % 2] if tg < NTG - 1 else None,
                              [c + i for c in cw], cb + i, 3)
            tv_th = conv_taps(S, C, Pv, av, tails[1][tg % 2], tails[1][(tg + 1) % 2] if tg < NTG - 1 else None,
                              [c + FFN // 128 + i for c in cw], cb + FFN // 128 + i, 3)
            if "c" not in SK:
                for a_, b_ in zip(tg_th, tv_th):
                    a_()
                    b_()
            ge = gel[u % 2] if 'I' not in SK else ag
            if 'D' in SK:
                ge = gel[tg]
            if "g" not in SK:
                if "G" not in SK:
                    act(S, V(ge), V(ag), AF.Gelu_apprx_tanh)
                elif "1" in SK:
                    act(S, V(ge), V(ag), AF.Identity)
                elif "2" in SK:
                    act(S, V(ge), V(C.lnt), AF.Tanh)
                elif "3" in SK:
                    act(S, V(ge), V(C.lnt), AF.Square)
                elif "4" in SK:
                    act(S, V(ge), V(C.lnt), AF.Exp)
                else:
                    act(S, V(ge), V(ag), AF.Tanh)
                if "t" not in SK:
                    tt(S, V(ab, j * T + tg * TG, j * T + (tg + 1) * TG), V(ge), V(av), ALU.mult)

    def down_group(g):
        s = g % 2
        for d in range(NCH):
            for tg in range(NTG):
                u = ucount[0]
                ucount[0] += 1
                P = V(C.PS[6 + (u % 2)], 0, TG)
                mm(S, P, [(V(wdb[s], j * D + d * 128, j * D + d * 128 + 128), V(abuf[s], j * T + tg * TG, j * T + (tg + 1) * TG)) for j in range(G)])
                hv = V(C.H, d * T + tg * TG, d * T + (tg + 1) * TG)
                tt(S, hv, P, hv, ALU.add)

    import os
    lvl = int(os.environ.get("FFN_LVL", "9"))
    if lvl == 0:
        S.release(m0)
        return
    nchk = int(os.environ.get("FFN_NCH", "24"))
    dodown = int(os.environ.get("FFN_DOWN", "1"))
    NG = (nchk + G - 1) // G
    for g in range(NG):
        s = g % 2
        if dodown:
            dma(S, "pool", wdb[s].t[:, :], wdn.t.ap()[g], ("ffn_wd", s), writes=[wdb[s].reg()])
        for j in range(G):
            if g * G + j < nchk:
                up_chunk(g * G + j, j, abuf[s])
        if g >= 1 and dodown:
            down_group(g - 1)
    if dodown:
        down_group(NG - 1)
    S.release(m0)
def rec_phase(S, C, win, wax, wout, hprev, hfin):
    XW = HALO + T
    nh = 3
    m0 = S.mark()
    m1 = S.alloc("rec_m1", NCH * T, BF16)
    m2 = S.alloc("rec_m2", NCH * T, BF16)
    mk_xn = S.mark()
    XN = S.alloc("rec_xn", NCH * XW, BF16)
    wgx = [S.alloc("rec_wgx%d" % i, NCH * 256, BF16) for i in range(2)]
    wab = [S.alloc("rec_wab%d" % i, 256, BF16) for i in range(2)]
    f = lambda nm: S.alloc("rec_" + nm, TG, F32)
    xc, gg, rr, ii, aa, ss, hl, Ac, zz = [f(n) for n in ("xc", "gg", "r", "i", "a", "s", "hl", "Ac", "zz")]
    xcb = S.alloc("rec_xcb", TG, BF16)
    tails = [S.alloc("rec_tl%d" % i, 4, F32) for i in range(2)]
    small = S.alloc("rec_small", 64, F32)
    memset(S, V(zz), 0.0)
    src = lambda c, c0, n: (V(C.HH, c * HALO + c0 + HALO, c * HALO + c0 + HALO + n) if c0 < 0
                            else V(C.H, c * T + c0, c * T + c0 + n))
    dst = lambda c, c0, n: V(XN, c * XW + HALO + c0, c * XW + HALO + c0 + n)
    rmsnorm(S, C, src, dst, C.VEC["rec_norm"], [(-nh, nh)] + [(tg * TG, TG) for tg in range(NTG)])
    ap0 = C.VEC["a_param"]
    act(S, V(small, 0, 8), V(C.vecs, ap0, ap0 + 8), AF.Exp, scale=-1.0)
    act(S, V(small, 0, 8), V(small, 0, 8), AF.Ln, bias=1.0)
    ts(S, V(small, 8, 16), V(small, 0, 8), -8.0, None, ALU.mult)
    ts(S, V(small, 16, 24), V(small, 0, 8), -16.0, None, ALU.mult)
    cw = [C.VEC["rec_cw_%d" % k] for k in range(4)]
    cb = C.VEC["rec_cb"]
    u = 0
    for n in range(NCH):
        W = wgx[n % 2]
        dma(S, "pool", W.t[:, :], win.t.ap()[n], ("rec_wgx", n % 2), writes=[W.reg()])
        WA = wab[n % 2]
        dma(S, "pool", WA.t[:, :], wax.t.ap()[n], ("rec_wab", n % 2), writes=[WA.reg()])
        ph = V(C.PS[4], 0, nh)
        mm(S, ph, [(V(W, k * 256 + 128, k * 256 + 256), V(XN, k * XW + HALO - nh, k * XW + HALO)) for k in range(NCH)])
        cp(S, V(tails[0], 0, nh), ph)
        for tg in range(NTG):
            Pg = V(C.PS[0 + (u % 2)], 0, TG)
            Px = V(C.PS[2 + (u % 2)], 0, TG)
            u += 1
            for h, P in ((0, Pg), (1, Px)):
                mm(S, P, [(V(W, k * 256 + h * 128, k * 256 + h * 128 + 128), V(XN, k * XW + HALO + tg * TG, k * XW + HALO + (tg + 1) * TG)) for k in range(NCH)])
            for th in conv_taps(S, C, Px, xc, tails[tg % 2], tails[(tg + 1) % 2] if tg < NTG - 1 else None,
                                [c + n for c in cw], cb + n, 4):
                th()
            act(S, V(gg), Pg, AF.Gelu_apprx_tanh)
            cp(S, V(xcb), V(xc))
            Pr = V(C.PS[4], 0, TG)
            Pi = V(C.PS[5], 0, TG)
            mm(S, Pr, [(V(WA, 0, 128), V(xcb))])
            mm(S, Pi, [(V(WA, 128, 256), V(xcb))])
            act(S, V(rr), Pr, AF.Sigmoid, bias=V(C.vecs, C.VEC["ga_b"] + n, C.VEC["ga_b"] + n + 1))
            act(S, V(ii), Pi, AF.Sigmoid, bias=V(C.vecs, C.VEC["gx_b"] + n, C.VEC["gx_b"] + n + 1))
            act(S, V(aa), V(rr), AF.Exp, scale=V(small, 8 + n, 9 + n))
            act(S, V(ss), V(rr), AF.Exp, scale=V(small, 16 + n, 17 + n))
            act(S, V(ss), V(ss), AF.Sqrt, bias=1.0, scale=-1.0)
            tt(S, V(ii), V(ii), V(xc), ALU.mult)
            tt(S, V(ii), V(ii), V(ss), ALU.mult)
            if tg > 0:
                cp(S, V(small, 24, 25), V(hl, TG - 1, TG))
                cp(S, V(small, 25, 26), V(Ac, TG - 1, TG))
            ih = V(small, 24, 25) if tg > 0 else 0.0
            ia = V(small, 25, 26) if tg > 0 else 1.0
            S.op("dve", lambda e, ih=ih: e.tensor_tensor_scan(V(hl).ap(), V(aa).ap(), V(ii).ap(), _sc(ih), ALU.mult, ALU.add),
                 reads=_rd(V(aa), V(ii), ih), writes=[V(hl).reg()])
            S.op("dve", lambda e, ia=ia: e.tensor_tensor_scan(V(Ac).ap(), V(aa).ap(), V(zz).ap(), _sc(ia), ALU.mult, ALU.add),
                 reads=_rd(V(aa), V(zz), ia), writes=[V(Ac).reg()])
            tt(S, V(m1, n * T + tg * TG, n * T + (tg + 1) * TG), V(hl), V(gg), ALU.mult)
            tt(S, V(m2, n * T + tg * TG, n * T + (tg + 1) * TG), V(Ac), V(gg), ALU.mult)
        cp(S, V(C.hfin, n, n + 1), V(hl, TG - 1, TG))
    S.release(mk_xn)
    wo = S.alloc("rec_wo", NCH * D, BF16)
    wo2 = S.alloc("rec_wo2", NCH * D, BF16)
    for n in range(NCH):
        dma(S, "pool", wo.t[:, n * D:(n + 1) * D], wout.t.ap()[n], ("rec_wo", n), writes=[wo.reg(n * D, (n + 1) * D)])
        ts(S, V(wo2, n * D, (n + 1) * D), V(wo, n * D, (n + 1) * D), V(C.hprev, n, n + 1), None, ALU.mult)
    for d in range(NCH):
        for tg in range(NTG):
            P = V(C.PS[6 + (u % 2)], 0, TG)
            u += 1
            pairs = [(V(wo, n * D + d * 128, n * D + d * 128 + 128), V(m1, n * T + tg * TG, n * T + (tg + 1) * TG)) for n in range(NCH)]
            pairs += [(V(wo2, n * D + d * 128, n * D + d * 128 + 128), V(m2, n * T + tg * TG, n * T + (tg + 1) * TG)) for n in range(NCH)]
            mm(S, P, pairs)
            hv = V(C.H, d * T + tg * TG, d * T + (tg + 1) * TG)
            tt(S, hv, P, hv, ALU.add)
    S.release(m0)
LAM_INIT0 = 0.8 - 0.6 * math.exp(-0.3 * 0)
TK = 2 * T


def attn_consts(S, C, cst_d):
    C.cst = S.alloc("acst", ACST_N, F32)
    dma(S, "sp", C.cst.t[:, :], cst_d.t.ap(), "acst", writes=[C.cst.reg()])
    C.ident = S.alloc("ident", 128, BF16)
    C.bd64 = S.alloc("bd64", 128, BF16)
    C.ones_f = S.alloc("ones_f", 128, F32)
    cp(S, V(C.ident), V(C.cst, ACST["ident"], ACST["ident"] + 128))
    cp(S, V(C.bd64), V(C.cst, ACST["bd64"], ACST["bd64"] + 128))
    memset(S, V(C.ones_f), 1.0)
    sm = S.alloc("asmall", 16, F32)
    C.asm = sm
    vv = lambda nm: V(C.vecs, C.VEC[nm], C.VEC[nm] + 1)
    tt(S, V(sm, 0, 1), vv("lq1"), vv("lk1"), ALU.mult)
    tt(S, V(sm, 1, 2), vv("lq2"), vv("lk2"), ALU.mult)
    ps = V(C.PS[4], 0, 2)
    mm(S, ps, [(V(C.ones_f), V(sm, 0, 2))])
    act(S, V(sm, 2, 4), ps, AF.Exp)
    tt(S, V(sm, 4, 5), V(sm, 3, 4), V(sm, 2, 3), ALU.subtract)
    ts(S, V(sm, 5, 6), V(sm, 4, 5), -LAM_INIT0, None, ALU.add)
    ts(S, V(sm, 6, 7), vv("bsub"), 1.0 - LAM_INIT0, None, ALU.mult)
    C.nlam = V(sm, 5, 6)
    C.gsub = V(sm, 6, 7)


def qk_norm_store(S, C, P, n, gain, dst, ubuf):
    sq = V(C.sqb[ubuf % 3], 0, n)
    act(S, sq, P, AF.Square)
    st = V(C.PS[4 + (ubuf % 2)], 0, n)
    mm(S, st, [(V(C.bd64), sq)])
    lnt = V(C.alnt[ubuf % 2], 0, n)
    act(S, lnt, st, AF.Ln, bias=EPS, scale=1.0 / 64)
    act(S, lnt, lnt, AF.Exp, scale=-0.5)
    stt(S, dst, P, gain, lnt, ALU.mult, ALU.mult)


def attn_phase(S, C, xo_d, xh_d, win_d, wo_d, hout_d):
    m0 = S.mark()
    XN = S.alloc("at_xn", NCH * T, BF16)
    XNH = S.alloc("at_xnh", NCH * T, BF16)
    Y = S.alloc("at_y", NCH * T, BF16)
    C.alnt = [S.alloc("at_lnt%d" % i, TG, F32) for i in range(2)]
    m1 = S.mark()
    stage = S.alloc("at_stage", NCH * TG, F32)
    for (xd, dstb, nm) in ((xh_d, XNH, "h"), (xo_d, XN, "o")):
        for tg in range(NTG):
            for c in range(NCH):
                dma(S, "sp", stage.t[:, c * TG:(c + 1) * TG], xd.t.ap()[:, c * T + tg * TG:c * T + (tg + 1) * TG], ("at_stage", c),
                    writes=[stage.reg(c * TG, (c + 1) * TG)])
            rmsnorm(S, C, lambda c, c0, n: V(stage, c * TG, c * TG + n), lambda c, c0, n, dstb=dstb, tg=tg: V(dstb, c * T + tg * TG, c * T + tg * TG + n),
                    C.VEC["attn_norm"], [(0, TG)])
    S.release(m1)
    wqkv = [S.alloc("at_w%d" % i, NCH * 384, BF16) for i in range(2)]
    KT = S.alloc("at_kt", TK, BF16)
    QT = S.alloc("at_qt", T, BF16)
    VT = S.alloc("at_vt", TK, BF16)
    VA = S.alloc("at_va", 32 * 256, BF16)
    OD = [S.alloc("at_od%d" % i, T, F32) for i in range(2)]
    Ssb = [S.alloc("at_ssb%d" % i, TG, F32) for i in range(2)]
    Pt = [S.alloc("at_pt%d" % i, TG, BF16) for i in range(4)]
    tmp = [S.alloc("at_tmp%d" % i, TG, F32) for i in range(3)]
    b2 = [S.alloc("at_b2%d" % i, 256, F32) for i in range(2)]
    PSb = [C.PS[6].t[:, :].bitcast(BF16), C.PS[7].t[:, :].bitcast(BF16)]
    vv = lambda nm: V(C.vecs, C.VEC[nm], C.VEC[nm] + 1)
    padneg = vv("padneg")
    uc = [0]

    def nxt():
        uc[0] += 1
        return uc[0]

    def project(g):
        W = wqkv[g % 2]
        dma(S, "pool", W.t[:, :], win_d.t.ap()[g], ("at_w", g % 2), writes=[W.reg()])
        gq, gk = (vv("aq"), vv("ak")) if g < 4 else (vv("bq"), vv("bk"))
        for tgk in range(2 * NTG):
            srcb, t0 = (XNH, tgk * TG) if tgk < NTG else (XN, (tgk - NTG) * TG)
            xs = lambda k: V(srcb, k * T + t0, k * T + t0 + TG)
            u = nxt()
            P = V(C.PS[u % 4], 0, TG)
            mm(S, P, [(V(W, k * 384 + 128, k * 384 + 256), xs(k)) for k in range(NCH)])
            qk_norm_store(S, C, P, TG, gk, V(KT, tgk * TG, (tgk + 1) * TG), u)
            u = nxt()
            P = V(C.PS[u % 4], 0, TG)
            mm(S, P, [(V(W, k * 384 + 256, k * 384 + 384), xs(k)) for k in range(NCH)])
            cp(S, V(VT, tgk * TG, (tgk + 1) * TG), P, eng="act")
            if tgk >= NTG:
                u = nxt()
                P = V(C.PS[u % 4], 0, TG)
                mm(S, P, [(V(W, k * 384, k * 384 + 128), xs(k)) for k in range(NCH)])
                qk_norm_store(S, C, P, TG, gq, V(QT, t0, t0 + TG), u)

    def v_blocks(d, jmin, aug):
        nb = 32 // d
        for r in range(d):
            for j in range(jmin, nb):
                slot = r * nb + j
                u = nxt()
                pb = PSb[u % 2]
                c0 = 128 * j * d + r
                src = V(VT, c0, c0 + 127 * d + 1, step=d)
                outp = pb[:, 0:128]
                S.op("pe", lambda e, outp=outp, src=src: e.transpose(outp, src.ap(), C.ident.t[:, :]),
                     reads=[src.reg(), C.ident.reg()], writes=[C.PS[6 + (u % 2)].reg()])
                if aug:
                    dv = VA.t[:, slot * 256:(slot + 1) * 256].rearrange("p (h c) -> p h c", h=2)[:, :, 0:64]
                    sv = outp.rearrange("p (h c) -> p h c", h=2)
                    S.op("dve", lambda e, dv=dv, sv=sv: e.tensor_copy(dv, sv), reads=[C.PS[6 + (u % 2)].reg()],
                         writes=[VA.reg(slot * 256, (slot + 1) * 256)])
                else:
                    dv = VA.t[:, slot * 128:(slot + 1) * 128]
                    S.op("dve", lambda e, dv=dv, outp=outp: e.tensor_copy(dv, outp), reads=[C.PS[6 + (u % 2)].reg()],
                         writes=[VA.reg(slot * 128, (slot + 1) * 128)])

    def dilated_pair(g):
        ones3 = VA.t[:, :].rearrange("p (s h c) -> p s h c", h=2, c=128)[:, :, :, 64:128]
        S.op("pool", lambda e: e.memset(ones3, 1.0), writes=[VA.reg()])
        for b, (w_, d) in enumerate(DIL):
            nb = 32 // d
            jmin = 16 // d - 1
            v_blocks(d, jmin, True)
            for ph in range(2):
                hd = 2 * g + ph
                sl8 = -8.0 * SLOPES[hd] * d
                stt(S, V(b2[0]), V(C.cst, ACST["ddist"], ACST["ddist"] + 256), sl8, V(C.cst, ACST["dmask"], ACST["dmask"] + 256), ALU.mult, ALU.add)
                ts(S, V(b2[1], 0, 128), V(b2[0], 0, 128), padneg, None, ALU.add)
                cp(S, V(b2[1], 128, 256), V(b2[0], 128, 256))
                p0, p1 = ph * 64, ph * 64 + 64
                for r in range(d):
                    for j in range(16 // d, nb):
                        u = nxt()
                        Sp = C.PS[u % 4]
                        q0 = 128 * j * d + r - T
                        qv = V(QT, q0, q0 + 127 * d + 1, p0, p1, step=d)
                        for i2, jj in enumerate((j - 1, j)):
                            k0 = 128 * jj * d + r
                            kv = V(KT, k0, k0 + 127 * d + 1, p0, p1, step=d)
                            mm(S, V(Sp, i2 * 128, i2 * 128 + 128), [(kv, qv)])
                        hist = (j - 1) < 16 // d
                        sb = Ssb[u % 2]
                        tt(S, V(sb, 0, 256), V(Sp, 0, 256), V(b2[1 if hist else 0]), ALU.add)
                        pt = Pt[u % 4]
                        act(S, V(pt, 0, 256), V(sb, 0, 256), AF.Exp, scale=0.125)
                        Op = C.PS[4 + (u % 2)]
                        pairs = []
                        for i2, jj in enumerate((j - 1, j)):
                            slot = r * nb + jj
                            pairs.append((V(VA, slot * 256 + ph * 128, slot * 256 + ph * 128 + 128), V(pt, i2 * 128, i2 * 128 + 128)))
                        mm(S, V(Op, 0, 128), pairs)
                        ov = V(OD[ph], q0, q0 + 127 * d + 1, step=d)
                        if b == 0:
                            cp(S, ov, V(Op, 0, 128))
                        else:
                            tt(S, ov, V(Op, 0, 128), ov, ALU.add)
        for ph in range(2):
            for tg in range(NTG):
                rc = V(tmp[tg % 2], 0, TG, 0, 64)
                S.op("dve", lambda e, rc=rc, ph=ph, tg=tg: e.reciprocal(rc.ap(), OD[ph].t[64:128, tg * TG:(tg + 1) * TG]),
                     reads=[OD[ph].reg(tg * TG, (tg + 1) * TG, 64, 128)], writes=[rc.reg()])
                tt(S, V(Y, g * T + tg * TG, g * T + (tg + 1) * TG, ph * 64, ph * 64 + 64), V(OD[ph], tg * TG, (tg + 1) * TG, 0, 64), rc, ALU.mult)

    def diff_head(h):
        g = 4 + h
        v_blocks(1, 0, False)
        sl = SLOPES[8 + h]
        for G in range(NTG):
            q0 = G * TG
            nkb = 16 + 4 * G + 4
            O = [C.PS[4], C.PS[5]]
            Dn = [C.PS[6], C.PS[7]]
            for kb in range(nkb):
                jj = kb - (16 + 4 * G)
                bcol = ACST["dbias"] + (h * 2 + (1 if kb < 16 else 0)) * 36 + (kb - 4 * G + 12)
                bias = V(C.cst, bcol, bcol + 1)
                for c in range(2):
                    u = nxt()
                    Sp = V(C.PS[u % 4], 0, TG)
                    mm(S, Sp, [(V(KT, kb * 128, kb * 128 + 128, c * 64, c * 64 + 64), V(QT, q0, q0 + TG, c * 64, c * 64 + 64))])
                    pt = V(Pt[u % 4])
                    if jj < 0:
                        act(S, pt, Sp, AF.Exp, bias=bias, scale=0.125)
                    else:
                        sb = Ssb[u % 2]
                        if jj > 0:
                            memset(S, V(sb, 0, jj * 128), 8 * NEG, eng="pool")
                        tt(S, V(sb, jj * 128, jj * 128 + 128), V(Sp.buf, jj * 128, jj * 128 + 128), V(C.cst, ACST["tri8"], ACST["tri8"] + 128), ALU.add)
                        if jj < 3:
                            cp(S, V(sb, jj * 128 + 128, TG), V(Sp.buf, jj * 128 + 128, TG))
                        act(S, pt, V(sb), AF.Exp, bias=bias, scale=0.125)
                    mm(S, V(O[c], 0, TG), [(V(VA, kb * 128, kb * 128 + 128), pt)], start=(kb == 0), stop=(kb == nkb - 1))
                    mm(S, V(Dn[c], 0, TG), [(V(C.ones_bf), pt)], start=(kb == 0), stop=(kb == nkb - 1))
            t0_, t1_, t2_ = V(tmp[0]), V(tmp[1]), V(tmp[2])
            S.op("dve", lambda e: e.reciprocal(t0_.ap(), Dn[0].t[:, 0:TG]), reads=[Dn[0].reg()], writes=[t0_.reg()])
            tt(S, t0_, V(O[0], 0, TG), t0_, ALU.mult)
            S.op("dve", lambda e: e.reciprocal(t1_.ap(), Dn[1].t[:, 0:TG]), reads=[Dn[1].reg()], writes=[t1_.reg()])
            tt(S, t1_, V(O[1], 0, TG), t1_, ALU.mult)
            stt(S, t0_, t1_, C.nlam, t0_, ALU.mult, ALU.add)
            sq = V(C.sqb[0], 0, TG)
            act(S, sq, t0_, AF.Square)
            st = V(C.PS[0], 0, TG)
            mm(S, st, [(V(C.ones_bf), sq)])
            act(S, t2_, st, AF.Ln, bias=EPS, scale=1.0 / 128)
            act(S, t2_, t2_, AF.Exp, scale=-0.5)
            stt(S, V(Y, g * T + q0, g * T + q0 + TG), t0_, C.gsub, t2_, ALU.mult, ALU.mult)

    for g in range(8):
        project(g)
        if g < 4:
            dilated_pair(g)
        else:
            diff_head(g - 4)
    if getattr(C, "ydbg", None) is not None:
        dma(S, "sp", C.ydbg.t.ap(), Y.t[:, :], "ydbg", reads=[Y.reg()], writes=[C.ydbg.reg()])
    wo = S.alloc("at_wo", NCH * D, BF16) if False else None
    S.release(m0)
    m2 = S.mark()
    Yk = Y
    S.sp_ = Y.off + NCH * T * 2
    wo = S.alloc("at_wo", NCH * D, BF16)
    xr_ = [S.alloc("at_xr%d" % i, TG, F32) for i in range(2)]
    rs_ = [S.alloc("at_rs%d" % i, TG, F32) for i in range(2)]
    for n in range(NCH):
        dma(S, "pool", wo.t[:, n * D:(n + 1) * D], wo_d.t.ap()[n], ("at_wo", n), writes=[wo.reg(n * D, (n + 1) * D)])
    for dch in range(NCH):
        for tg in range(NTG):
            u = nxt()
            P = V(C.PS[u % 4], 0, TG)
            mm(S, P, [(V(wo, k * D + dch * 128, k * D + dch * 128 + 128), V(Y, k * T + tg * TG, k * T + (tg + 1) * TG)) for k in range(NCH)])
            xr = xr_[u % 2]
            rs = rs_[u % 2]
            dma(S, "sp", xr.t[:, :], xo_d.t.ap()[:, dch * T + tg * TG:dch * T + (tg + 1) * TG], ("at_xr", u % 2), writes=[xr.reg()])
            tt(S, V(rs), P, V(xr), ALU.add)
            dma(S, "sp", hout_d.t.ap()[:, dch * T + tg * TG:dch * T + (tg + 1) * TG], rs.t[:, :], ("hout", u % 2),
                reads=[rs.reg()], writes=[hout_d.reg(dch * T + tg * TG, dch * T + (tg + 1) * TG)])
    S.release(m0)
ARENA_BYTES = 204 * 1024


class Ctx:
    pass


def make_ctx(S, with_H=True):
    C = Ctx()
    C.VEC = VEC
    C.PS = [S.psum("ps%d" % i) for i in range(8)]
    C.vecs = S.alloc("vecs", NV, F32)
    C.ones_bf = S.alloc("ones_bf", 128, BF16)
    C.sqb = [S.alloc("sqb%d" % i, TG, BF16) for i in range(3)]
    C.lnt = S.alloc("lnt", TG, F32)
    if with_H:
        C.H = S.alloc("H", NCH * T, F32)
    C.HH = S.alloc("HH", NCH * HALO, F32)
    C.hprev = S.alloc("hprev", 8, F32)
    C.hfin = S.alloc("hfin", 8, F32)
    return C


def load_common(S, C, vecs_d):
    dma(S, "sp", C.vecs.t[:, :], vecs_d.t.ap(), "vecs", writes=[C.vecs.reg()])
    memset(S, V(C.ones_bf), 1.0)


def load_H(S, C, hin_d, halo_d):
    for c in range(NCH):
        dma(S, "sp", C.H.t[:, c * T:(c + 1) * T], hin_d.t.ap()[:, c * T:(c + 1) * T], ("hin", c), writes=[C.H.reg(c * T, (c + 1) * T)])
    dma(S, "sp", C.HH.t[:, :], halo_d.t.ap(), "halo", writes=[C.HH.reg()])


def store_H(S, C, hout_d):
    for c in range(NCH):
        dma(S, "sp", hout_d.t.ap()[:, c * T:(c + 1) * T], C.H.t[:, c * T:(c + 1) * T], "hout",
            reads=[C.H.reg(c * T, (c + 1) * T)], writes=[hout_d.reg(c * T, (c + 1) * T)])


def build_ffn_program(L):
    nc = bass.Bass("TRN2", target_bir_lowering=False)
    S = Sched(nc, ARENA_BYTES)
    C = make_ctx(S)
    vecs_d = S.dram("vecs", [128, NV], F32, "ExternalInput")
    hin_d = S.dram("hin", [128, NCH * T], F32, "ExternalInput")
    halo_d = S.dram("halo", [128, NCH * HALO], F32, "ExternalInput")
    wup_d = S.dram("wup", [24, 128, NCH * 256], F32, "ExternalInput")
    wdn_d = S.dram("wdn", [6, 128, 4 * D], F32, "ExternalInput")
    hout_d = S.dram("hout", [128, NCH * T], F32, "ExternalOutput")
    load_common(S, C, vecs_d)
    load_H(S, C, hin_d, halo_d)
    ffn_phase(S, C, L, wup_d, wdn_d)
    store_H(S, C, hout_d)
    S.emit(final_waits=["hout"])
    return nc, S


def build_rec_program():
    nc = bass.Bass("TRN2", target_bir_lowering=False)
    S = Sched(nc, ARENA_BYTES)
    C = make_ctx(S)
    vecs_d = S.dram("vecs", [128, NV], F32, "ExternalInput")
    hin_d = S.dram("hin", [128, NCH * T], F32, "ExternalInput")
    halo_d = S.dram("halo", [128, NCH * HALO], F32, "ExternalInput")
    hprev_d = S.dram("hprev", [128, 8], F32, "ExternalInput")
    win_d = S.dram("win", [8, 128, NCH * 256], F32, "ExternalInput")
    wax_d = S.dram("wax", [8, 128, 256], F32, "ExternalInput")
    wout_d = S.dram("wout", [8, 128, D], F32, "ExternalInput")
    hout_d = S.dram("hout", [128, NCH * T], F32, "ExternalOutput")
    hfin_d = S.dram("hfin", [128, 8], F32, "ExternalOutput")
    load_common(S, C, vecs_d)
    load_H(S, C, hin_d, halo_d)
    dma(S, "sp", C.hprev.t[:, :], hprev_d.t.ap(), "hprev", writes=[C.hprev.reg()])
    rec_phase(S, C, win_d, wax_d, wout_d, None, None)
    store_H(S, C, hout_d)
    dma(S, "sp", hfin_d.t.ap(), C.hfin.t[:, :], "hfin", reads=[C.hfin.reg()], writes=[hfin_d.reg()])
    S.emit(final_waits=["hout", "hfin"])
    return nc, S


def build_attn_program():
    nc = bass.Bass("TRN2", target_bir_lowering=False)
    S = Sched(nc, ARENA_BYTES)
    C = make_ctx(S, with_H=False)
    vecs_d = S.dram("vecs", [128, NV], F32, "ExternalInput")
    cst_d = S.dram("acst", [128, ACST_N], F32, "ExternalInput")
    xo_d = S.dram("xo", [128, NCH * T], F32, "ExternalInput")
    xh_d = S.dram("xh", [128, NCH * T], F32, "ExternalInput")
    win_d = S.dram("win", [8, 128, NCH * 384], F32, "ExternalInput")
    wo_d = S.dram("wo", [8, 128, D], F32, "ExternalInput")
    hout_d = S.dram("hout", [128, NCH * T], F32, "ExternalOutput")
    load_common(S, C, vecs_d)
    attn_consts(S, C, cst_d)
    fw = [("hout", 0), ("hout", 1)]
    attn_phase(S, C, xo_d, xh_d, win_d, wo_d, hout_d)
    S.emit(final_waits=fw)
    return nc, S


_PROG = {}


def _prog(name, builder):
    if name not in _PROG:
        _PROG[name] = builder()[0]
    return _PROG[name]


def _halos(h_cores):
    out = []
    for c in range(8):
        if c % 2 == 0:
            out.append(np.zeros((128, NCH * HALO), np.float32))
        else:
            prev = h_cores[c - 1].reshape(128, NCH, T)
            out.append(np.ascontiguousarray(prev[:, :, T - HALO:].reshape(128, NCH * HALO)))
    return out


def kernel(**inputs):
    inp = {k: np.asarray(v) for k, v in inputs.items()}
    x = inp["x"]
    cores = list(range(8))
    vecs = [build_vecs(inp, c % 2) for c in cores]
    nc = _prog("attn", build_attn_program)
    win = lay_attn_win(inp["attn_w_in"][0])
    wo = np.ascontiguousarray(inp["attn_w_out"][0].reshape(8, 128, D))
    acst = [build_attn_consts(0), build_attn_consts(1)]
    in_maps = []
    for c in cores:
        b, half = c // 2, c % 2
        xo = lay_T(x[b, half * T:(half + 1) * T])
        xh = lay_T(x[b, 0:T]) if half == 1 else np.zeros((128, NCH * T), np.float32)
        in_maps.append({"vecs": vecs[c], "acst": acst[half], "xo": xo, "xh": xh, "win": win, "wo": wo})
    res = run_bass_kernel_spmd(nc, in_maps, core_ids=cores).results
    h = [res[c]["hout"] for c in cores]

    def ffn(L, h):
        nc = _prog("ffn%d" % L, lambda: build_ffn_program(L))
        wup = lay_wup(inp["ffn_w_up"][L])
        wdn = lay_wdn(inp["ffn_w_down"][L])
        hl = _halos(h)
        maps = [{"vecs": vecs[c], "hin": h[c], "halo": hl[c], "wup": wup, "wdn": wdn} for c in cores]
        r = run_bass_kernel_spmd(nc, maps, core_ids=cores).results
        return [r[c]["hout"] for c in cores]

    h = ffn(0, h)
    nc = _prog("rec", build_rec_program)
    rwin = lay_win(inp["rec_w_in"][0])
    rwax = lay_wax(inp["rec_gate_a_w"][0], inp["rec_gate_x_w"][0])
    rwo = np.ascontiguousarray(inp["rec_w_out"][0].reshape(8, 128, D))
    hl = _halos(h)
    z = np.zeros((128, 8), np.float32)

    def rec(hprev):
        maps = [{"vecs": vecs[c], "hin": h[c], "halo": hl[c], "hprev": hprev[c], "win": rwin, "wax": rwax, "wout": rwo} for c in cores]
        return run_bass_kernel_spmd(nc, maps, core_ids=cores).results
    r1 = rec([z] * 8)
    r2 = rec([z if c % 2 == 0 else r1[c - 1]["hfin"] for c in cores])
    h = [r2[c]["hout"] for c in cores]
    h = ffn(1, h)
    out = np.zeros_like(x)
    for c in cores:
        out[c // 2, (c % 2) * T:(c % 2 + 1) * T] = unlay_T(h[c], T)
    return out
```

```python
import math
import numpy as np
import concourse.bass as bass
import concourse.mybir as mybir
from concourse.bass_utils import run_bass_kernel_spmd

F32 = mybir.dt.float32
BF16 = mybir.dt.bfloat16
AF = mybir.ActivationFunctionType
ALU = mybir.AluOpType
ENGS = ("pe", "act", "dve", "pool", "sp")
EMBED_WAITS = True

D = 1024
T = 2048
TG = 512
NTG = 4
NCH = 8
HALO = 4
FFN = 3072
NEG = -30000.0
EPS = 1e-6
SLOPES = [2.0 ** (-8.0 * (i + 1) / 12.0) for i in range(12)]
DIL = ((128, 1), (512, 4), (2048, 16))


class Buf:
    def __init__(self, t, space, nparts, ncols, esize, off=0):
        self.t, self.space, self.nparts, self.ncols, self.esize, self.off = t, space, nparts, ncols, esize, off

    def reg(self, c0=0, c1=None, p0=0, p1=None):
        if c1 is None:
            c1 = self.ncols
        if p1 is None:
            p1 = self.nparts
        if self.space[0] == "p":
            return (self.space, 0, 128, 0, 2048)
        return (self.space, p0, p1, self.off + c0 * self.esize, self.off + c1 * self.esize)


class V:
    def __init__(self, buf, c0=0, c1=None, p0=0, p1=None, step=1):
        self.buf = buf
        self.c0 = c0
        self.c1 = buf.ncols if c1 is None else c1
        self.p0 = p0
        self.p1 = buf.nparts if p1 is None else p1
        self.step = step

    def ap(self):
        if self.step == 1:
            return self.buf.t[self.p0:self.p1, self.c0:self.c1]
        return self.buf.t[self.p0:self.p1, self.c0:self.c1:self.step]

    def reg(self):
        return self.buf.reg(self.c0, self.c1, self.p0, self.p1)


def _overlap(a, b):
    return a[0] == b[0] and a[1] < b[2] and b[1] < a[2] and a[3] < b[4] and b[3] < a[4]


def _covers(a, b):
    return a[0] == b[0] and a[1] <= b[1] and a[2] >= b[2] and a[3] <= b[3] and a[4] >= b[4]


class Op:
    __slots__ = ("eng", "idx", "fn", "deps", "dma_key", "dma_val", "signal", "clock", "gseq", "dma_inc")


class Sched:
    def __init__(self, nc, arena_bytes):
        self.nc = nc
        self.ops = {e: [] for e in ENGS}
        self.acc = {}
        self.dma_keys = {}
        self.sems = {e: nc.alloc_semaphore("s_" + e) for e in ENGS}
        self._ctr = 0
        self._gseq = 0
        self.arena = nc.alloc_sbuf_tensor("arena", [128, arena_bytes // 4], F32)
        self.arena_bytes = arena_bytes
        self.sp_ = 0
        self.hiwater = 0

    def alloc(self, name, ncols, dtype):
        es = 2 if dtype == BF16 else 4
        nbytes = (ncols * es + 63) // 64 * 64
        off = self.sp_
        self.sp_ += nbytes
        self.hiwater = max(self.hiwater, self.sp_)
        assert self.sp_ <= self.arena_bytes, "SBUF arena overflow %s: %d > %d" % (name, self.sp_, self.arena_bytes)
        ap = self.arena[:, off // 4:(off + nbytes) // 4]
        if dtype == BF16:
            ap = ap.bitcast(BF16)
        ap = ap[:, 0:ncols]
        return Buf(ap, "sb", 128, ncols, es, off)

    def alloc_at(self, off, ncols, dtype):
        es = 2 if dtype == BF16 else 4
        nbytes = (ncols * es + 63) // 64 * 64
        ap = self.arena[:, off // 4:(off + nbytes) // 4]
        if dtype == BF16:
            ap = ap.bitcast(BF16)
        ap = ap[:, 0:ncols]
        return Buf(ap, "sb", 128, ncols, es, off)

    def mark(self):
        return self.sp_

    def release(self, m):
        self.sp_ = m

    def psum(self, name, ncols=512, dtype=F32):
        t = self.nc.alloc_psum_tensor(name, [128, ncols], dtype)
        self._ctr += 1
        return Buf(t, "ps%d" % self._ctr, 128, ncols, 4 if dtype == F32 else 2)

    def dram(self, name, shape, dtype, kind):
        t = self.nc.dram_tensor(name, list(shape), dtype, kind=kind)
        self._ctr += 1
        es = 2 if dtype == BF16 else 4
        ncols = int(np.prod(shape[1:])) if len(shape) > 1 else 1
        return Buf(t, "dr%d" % self._ctr, shape[0], ncols, es)

    def op(self, eng, fn, reads=(), writes=(), dma_key=None, dma_inc=16):
        o = Op()
        o.eng, o.fn = eng, fn
        o.idx = len(self.ops[eng])
        self._gseq += 1
        o.gseq = self._gseq
        o.signal = False
        o.dma_key = dma_key
        o.dma_val = None
        if dma_key is not None:
            if dma_key not in self.dma_keys:
                self.dma_keys[dma_key] = [self.nc.alloc_semaphore("d_%d" % len(self.dma_keys)), 0]
            self.dma_keys[dma_key][1] += dma_inc
            o.dma_val = self.dma_keys[dma_key][1]
        o.dma_inc = dma_inc
        best = {}

        def consider(d):
            k = ("dma", d.dma_key) if d.dma_key is not None else ("eng", d.eng)
            v = d.dma_val if d.dma_key is not None else d.idx
            if k not in best or v > best[k][0]:
                best[k] = (v, d)

        for r in reads:
            for (reg, op_, w) in self.acc.get(r[0], ()):
                if w and _overlap(reg, r):
                    consider(op_)
        for wr in writes:
            for (reg, op_, w) in self.acc.get(wr[0], ()):
                if _overlap(reg, wr):
                    consider(op_)
        o.deps = [x[1] for x in best.values()]
        self.ops[eng].append(o)
        for r in reads:
            self.acc.setdefault(r[0], []).append((r, o, False))
        for wr in writes:
            lst = self.acc.setdefault(wr[0], [])
            lst[:] = [x for x in lst if not _covers(wr, x[0])]
            lst.append((wr, o, True))
        return o

    def emit(self, final_waits=()):
        nc = self.nc
        eidx = {e: i for i, e in enumerate(ENGS)}
        needed = {}
        cur = {e: [-1] * len(ENGS) for e in ENGS}
        dma_waited = {e: {} for e in ENGS}
        allops = []
        for e in ENGS:
            allops.extend(self.ops[e])
        allops.sort(key=lambda o: o.gseq)
        nwaits = 0
        for o in allops:
            e = o.eng
            clk = cur[e]
            waits = []
            for d in o.deps:
                if d.dma_key is not None:
                    if dma_waited[e].get(d.dma_key, 0) < d.dma_val:
                        dma_waited[e][d.dma_key] = d.dma_val
                        waits.append(("dma", d.dma_key, d.dma_val))
                    dc = d.clock
                    for i in range(len(ENGS)):
                        if dc[i] > clk[i]:
                            clk[i] = dc[i]
                    continue
                fi = eidx[d.eng]
                if d.eng == e and e == "pe":
                    continue
                if clk[fi] >= d.idx:
                    continue
                d.signal = True
                waits.append(("eng", d.eng, d))
                dc = d.clock
                for i in range(len(ENGS)):
                    if dc[i] > clk[i]:
                        clk[i] = dc[i]
                clk[fi] = max(clk[fi], d.idx)
            needed[o] = waits
            nwaits += len(waits)
            o.clock = list(clk)
        cnt = {}
        for e in ENGS:
            c = 0
            for o in self.ops[e]:
                if o.signal:
                    c += 1
                cnt[o] = c
        self.stats = {e: len(self.ops[e]) for e in ENGS}
        self.stats["waits"] = nwaits

        def run_engine(e, engobj):
            for o in self.ops[e]:
                ws = [(self.dma_keys[w[1]][0], w[2]) if w[0] == "dma" else (self.sems[w[1]], cnt[w[2]]) for w in needed[o]]
                embed = EMBED_WAITS and o.dma_key is None and len(ws) > 0
                for (sm, vl) in (ws[:-1] if embed else ws):
                    engobj.wait_ge(sm, vl)
                ins = o.fn(engobj)
                if isinstance(ins, tuple):
                    first, ins = ins
                else:
                    first = ins
                if embed:
                    first._wait_ge(ws[-1][0], ws[-1][1])
                if o.dma_key is not None:
                    ins.then_inc(self.dma_keys[o.dma_key][0], o.dma_inc)
                elif o.signal:
                    ins.then_inc(self.sems[e], 1)
            if e == "sp":
                for k in final_waits:
                    engobj.wait_ge(self.dma_keys[k][0], self.dma_keys[k][1])

        with nc.Block() as block:
            @block.tensor
            def _(eng):
                run_engine("pe", eng)

            @block.scalar
            def _(eng):
                run_engine("act", eng)

            @block.vector
            def _(eng):
                run_engine("dve", eng)

            @block.gpsimd
            def _(eng):
                run_engine("pool", eng)

            @block.sync
            def _(eng):
                run_engine("sp", eng)


def _sc(x):
    return x.ap() if isinstance(x, V) else x


def _rd(*xs):
    return [x.reg() for x in xs if isinstance(x, V)]


def mm(S, out, pairs, start=True, stop=True):
    pairs = list(pairs)

    def fn(e):
        n = len(pairs)
        ins = None
        first = None
        for i, (l, r) in enumerate(pairs):
            ins = e.matmul(out.ap(), lhsT=l.ap(), rhs=r.ap(), start=(start and i == 0), stop=(stop and i == n - 1))
            if first is None:
                first = ins
        return (first, ins)
    rs = []
    for l, r in pairs:
        rs.append(l.reg())
        rs.append(r.reg())
    return S.op("pe", fn, reads=rs, writes=[out.reg()])


import os as _os
_PRELOAD = _os.environ.get("PRELOAD", "0") == "1"


def act(S, out, in_, func, bias=0.0, scale=1.0):
    def fn(e):
        if _PRELOAD and func in (AF.Gelu_apprx_tanh, AF.Tanh):
            e.preload_act_table(func)
        return e.activation(out=out.ap(), in_=in_.ap(), func=func, bias=_sc(bias), scale=_sc(scale))
    return S.op("act", fn, reads=_rd(in_, bias, scale), writes=[out.reg()])


def ts(S, out, in0, s1, s2, op0, op1=None, eng="dve"):
    if op1 is None:
        return S.op(eng, lambda e: e.tensor_scalar(out.ap(), in0.ap(), _sc(s1), None, op0),
                    reads=_rd(in0, s1), writes=[out.reg()])
    return S.op(eng, lambda e: e.tensor_scalar(out.ap(), in0.ap(), _sc(s1), _sc(s2), op0, op1),
                reads=_rd(in0, s1, s2), writes=[out.reg()])


def stt(S, out, in0, s, in1, op0, op1):
    return S.op("dve", lambda e: e.scalar_tensor_tensor(out.ap(), in0.ap(), _sc(s), in1.ap(), op0, op1),
                reads=_rd(in0, s, in1), writes=[out.reg()])


def tt(S, out, in0, in1, op, eng="dve"):
    return S.op(eng, lambda e: e.tensor_tensor(out.ap(), in0.ap(), in1.ap(), op),
                reads=_rd(in0, in1), writes=[out.reg()])


def cp(S, out, in_, eng="dve"):
    if eng == "act":
        return act(S, out, in_, AF.Identity)
    return S.op(eng, lambda e: e.tensor_copy(out.ap(), in_.ap()), reads=_rd(in_), writes=[out.reg()])


def memset(S, out, val, eng="pool"):
    return S.op(eng, lambda e: e.memset(out.ap(), val), writes=[out.reg()])


def dma(S, q, out_ap, in_ap, key, reads=(), writes=()):
    if q == "pool":
        return S.op(q, lambda e: e.dma_start(out=out_ap, in_=in_ap, max_dma_last_dim=2048), reads=list(reads), writes=list(writes), dma_key=key)
    return S.op(q, lambda e: e.dma_start(out=out_ap, in_=in_ap), reads=list(reads), writes=list(writes), dma_key=key)


def load_w(S, wd, r0, nk, c0, ncols, dst, key, q="pool"):
    src = wd.t.ap()[r0:r0 + nk * 128, c0:c0 + ncols].rearrange("(kc p) n -> p kc n", p=128)
    d = dst.t[:, 0:nk * ncols].rearrange("p (kc n) -> p kc n", kc=nk)
    return dma(S, q, d, src, key, writes=[dst.reg(0, nk * ncols)])
def _vec_layout():
    VEC = {}
    n = [0]

    def add(name, w):
        VEC[name] = n[0]
        n[0] += w
    for nm in ("attn_norm", "ffn_norm0", "rec_norm", "ffn_norm1"):
        add(nm, 8)
    for L in range(2):
        for k in range(3):
            add("ffn_cw%d_%d" % (L, k), 48)
        add("ffn_cb%d" % L, 48)
    for k in range(4):
        add("rec_cw_%d" % k, 8)
    for nm in ("rec_cb", "ga_b", "gx_b", "a_param"):
        add(nm, 8)
    for nm in ("aq", "ak", "bq", "bk", "bsub", "lq1", "lk1", "lq2", "lk2", "padneg", "hsel"):
        add(nm, 1)
    add("sel", 8)
    return VEC, n[0]


VEC, NV = _vec_layout()


def _cols(v):
    v = np.asarray(v, np.float32).reshape(-1, 128)
    return np.ascontiguousarray(v.T)


def build_vecs(inp, half, core=None):
    out = np.zeros((128, NV), np.float32)

    def put(name, arr):
        out[:, VEC[name]:VEC[name] + arr.shape[1]] = arr
    put("attn_norm", _cols(inp["attn_norm"][0]))
    put("ffn_norm0", _cols(inp["ffn_norm"][0]))
    put("rec_norm", _cols(inp["rec_norm"][0]))
    put("ffn_norm1", _cols(inp["ffn_norm"][1]))
    for L in range(2):
        for k in range(3):
            put("ffn_cw%d_%d" % (L, k), _cols(inp["ffn_conv_w"][L, k]))
        put("ffn_cb%d" % L, _cols(inp["ffn_conv_b"][L]))
    for k in range(4):
        put("rec_cw_%d" % k, _cols(inp["rec_conv_w"][0, k]))
    put("rec_cb", _cols(inp["rec_conv_b"][0]))
    put("ga_b", _cols(inp["rec_gate_a_b"][0]))
    put("gx_b", _cols(inp["rec_gate_x_b"][0]))
    put("a_param", _cols(inp["rec_a_param"][0]))
    for nm, key in (("aq", "a_q_norm"), ("ak", "a_k_norm"), ("bq", "b_q_norm"), ("bk", "b_k_norm")):
        put(nm, np.tile(np.asarray(inp[key][0], np.float32), 2)[:, None])
    put("bsub", np.asarray(inp["b_sub_norm"][0], np.float32)[:, None])
    for nm, key in (("lq1", "b_lam_q1"), ("lk1", "b_lam_k1"), ("lq2", "b_lam_q2"), ("lk2", "b_lam_k2")):
        col = np.zeros((128, 1), np.float32)
        col[:64, 0] = np.asarray(inp[key][0], np.float32)
        put(nm, col)
    put("padneg", np.full((128, 1), 0.0 if half == 1 else NEG, np.float32))
    put("hsel", np.full((128, 1), 1.0 if half == 1 else 0.0, np.float32))
    selm = np.zeros((128, 8), np.float32)
    if core is not None and core % 2 == 1:
        selm[:, core - 1] = 1.0
    put("sel", selm)
    return out


def lay_wup(w):
    w = np.asarray(w, np.float32)
    g = w[:, :FFN].reshape(8, 128, 24, 128)
    v = w[:, FFN:].reshape(8, 128, 24, 128)
    gv = np.concatenate([g, v], axis=3)
    return np.ascontiguousarray(gv.transpose(2, 1, 0, 3).reshape(24, 128, 8 * 256))


def lay_wdn(w, G=4):
    w = np.asarray(w, np.float32).reshape(24 // G, G, 128, 1024)
    return np.ascontiguousarray(w.transpose(0, 2, 1, 3).reshape(24 // G, 128, G * 1024))


def lay_kmajor(w, ncols):
    w = np.asarray(w, np.float32)
    K, N = w.shape
    x = w.reshape(K // 128, 128, N // ncols, ncols)
    return np.ascontiguousarray(x.transpose(2, 1, 0, 3).reshape(N // ncols, 128, (K // 128) * ncols))


def lay_T(x):
    x = np.asarray(x, np.float32)
    t = x.shape[0]
    return np.ascontiguousarray(x.reshape(t, 8, 128).transpose(2, 1, 0).reshape(128, 8 * t))


def unlay_T(y, t):
    return np.ascontiguousarray(y.reshape(128, 8, t).transpose(2, 1, 0).reshape(t, 1024))


def lay_win(w):
    w = np.asarray(w, np.float32)
    g = w[:, :D].reshape(8, 128, 8, 128)
    x = w[:, D:].reshape(8, 128, 8, 128)
    gx = np.concatenate([g, x], axis=3)
    return np.ascontiguousarray(gx.transpose(2, 1, 0, 3).reshape(8, 128, 8 * 256))


def lay_wax(wa, wx):
    return np.ascontiguousarray(np.concatenate([np.asarray(wa, np.float32), np.asarray(wx, np.float32)], axis=2))


ACST = {}
def _acst_layout():
    n = 0
    for nm, w in (("ident", 128), ("bd64", 128), ("ddist", 256), ("dmask", 256), ("tri8", 128), ("dbias", 4 * 2 * 36)):
        ACST[nm] = n
        n += w
    return n
ACST_N = _acst_layout()


def build_attn_consts(half):
    c = np.zeros((128, ACST_N), np.float32)
    k = np.arange(128)[:, None].astype(np.float32)
    q = np.arange(128)[None, :].astype(np.float32)
    c[:, ACST["ident"]:ACST["ident"] + 128] = np.eye(128, dtype=np.float32)
    bd = np.zeros((128, 128), np.float32); bd[:64, :64] = 1; bd[64:, 64:] = 1
    c[:, ACST["bd64"]:ACST["bd64"] + 128] = bd
    dprev = 128 + q - k
    down = q - k
    c[:, ACST["ddist"]:ACST["ddist"] + 128] = np.where(k >= q, dprev, 0)
    c[:, ACST["ddist"] + 128:ACST["ddist"] + 256] = np.where(k <= q, down, 0)
    c[:, ACST["dmask"]:ACST["dmask"] + 128] = np.where(k >= q, 0, 8 * NEG)
    c[:, ACST["dmask"] + 128:ACST["dmask"] + 256] = np.where(k <= q, 0, 8 * NEG)
    c[:, ACST["tri8"]:ACST["tri8"] + 128] = np.where(k <= q, 0, 8 * NEG)
    for h in range(4):
        sl = SLOPES[8 + h]
        for pad in range(2):
            for idx in range(36):
                col = ACST["dbias"] + (h * 2 + pad) * 36 + idx
                v = sl * ((idx - 28) * 128 + k[:, 0])
                if pad and half == 0:
                    v = v + NEG
                c[:, col] = v
    return c


def lay_attn_win(w):
    w = np.asarray(w, np.float32)
    out = np.zeros((8, 128, 8, 384), np.float32)
    for g in range(8):
        if g < 4:
            cq, ck, cv = g * 128, 512 + g * 128, 1024 + g * 128
        else:
            hh = g - 4
            cq, ck, cv = 1536 + hh * 128, 2048 + hh * 128, 2560 + hh * 128
        for i, c0 in enumerate((cq, ck, cv)):
            out[g, :, :, i * 128:(i + 1) * 128] = w[:, c0:c0 + 128].reshape(8, 128, 128).transpose(1, 0, 2)
    return np.ascontiguousarray(out.reshape(8, 128, 8 * 384))
import os
def rmsnorm(S, C, src, dst, gcol, groups, nfeat_inv=1.0 / D):
    for (c0, n) in groups:
        st = V(C.PS[4], 0, n)
        for c in range(NCH):
            sq = V(C.sqb[c % 3], 0, n)
            act(S, sq, src(c, c0, n), AF.Square)
            mm(S, st, [(V(C.ones_bf), sq)], start=(c == 0), stop=(c == NCH - 1))
        lnt = V(C.lnt, 0, n)
        act(S, lnt, st, AF.Ln, bias=EPS, scale=nfeat_inv)
        rs = V(C.PS[5], 0, n)
        act(S, rs, lnt, AF.Exp, scale=-0.5)
        for c in range(NCH):
            stt(S, dst(c, c0, n), src(c, c0, n), V(C.vecs, gcol + c, gcol + c + 1), rs, ALU.mult, ALU.mult)


def conv_taps(S, C, P, acc, tail_prev, tail_new, wcols, bcol, K, n=TG):
    vw = lambda k: V(C.vecs, wcols[k], wcols[k] + 1)
    act(S, V(acc, 0, n), V(P.buf, P.c0, P.c0 + n), AF.Identity, bias=V(C.vecs, bcol, bcol + 1), scale=vw(K - 1))
    thunks = []
    for s in range(1, K):
        k = K - 1 - s
        thunks.append(lambda s=s, k=k: stt(S, V(acc, s, n), V(P.buf, P.c0, P.c0 + n - s), vw(k), V(acc, s, n), ALU.mult, ALU.add))
        thunks.append(lambda s=s, k=k: stt(S, V(acc, 0, s), V(tail_prev, K - 1 - s, K - 1), vw(k), V(acc, 0, s), ALU.mult, ALU.add))
    if tail_new is not None:
        thunks.append(lambda: cp(S, V(tail_new, 0, K - 1), V(P.buf, P.c0 + n - (K - 1), P.c0 + n), eng="dve"))
    return thunks


def ffn_phase(S, C, L, wup, wdn):
    XW = HALO + T
    nh = 2
    m0 = S.mark()
    XN = S.alloc("ffn_xn", NCH * XW, BF16)
    G = 4
    NG = (FFN // 128) // G
    abuf = [S.alloc("ffn_a%d" % i, G * T, BF16) for i in range(2)]
    wdb = [S.alloc("ffn_wd%d" % i, G * D, BF16) for i in range(2)]
    wgv = [S.alloc("ffn_wgv%d" % i, NCH * 256, BF16) for i in range(3)]
    accg = [S.alloc("ffn_accg%d" % i, TG, F32) for i in range(2)]
    accv = [S.alloc("ffn_accv%d" % i, TG, F32) for i in range(2)]
    gel = [S.alloc("ffn_gel%d" % i, TG, F32) for i in range(4)]
    tails = [[S.alloc("ffn_tl%d%d" % (h, i), 2, F32) for i in range(2)] for h in range(2)]
    gname = "ffn_norm%d" % L
    src = lambda c, c0, n: (V(C.HH, c * HALO + c0 + HALO, c * HALO + c0 + HALO + n) if c0 < 0
                            else V(C.H, c * T + c0, c * T + c0 + n))
    dst = lambda c, c0, n: V(XN, c * XW + HALO + c0, c * XW + HALO + c0 + n)
    import os
    if int(os.environ.get('FFN_LVL', '9')) >= 0 and 'N' not in os.environ.get('SKIP', ''):
        rmsnorm(S, C, src, dst, C.VEC[gname], [(-nh, nh)] + [(tg * TG, TG) for tg in range(NTG)])
    cw = [C.VEC["ffn_cw%d_%d" % (L, k)] for k in range(3)]
    cb = C.VEC["ffn_cb%d" % L]
    ucount = [0]

    def up_chunk(i, j, ab):
        slot = i % 3
        W = wgv[slot]
        dma(S, "pool", W.t[:, :], wup.t.ap()[i], ("ffn_wgv", slot), writes=[W.reg()])
        SK = os.environ.get("SKIP", "")
        for h in range(2):
            if "h" in SK:
                memset(S, V(tails[h][0]), 0.0)
                continue
            ph = V(C.PS[4 + h], 0, nh)
            mm(S, ph, [(V(W, k * 256 + h * 128, k * 256 + h * 128 + 128), V(XN, k * XW + HALO - nh, k * XW + HALO)) for k in range(NCH)])
            cp(S, V(tails[h][0], 0, nh), ph, eng="dve")
        for tg in range(NTG):
            u = ucount[0]
            ucount[0] += 1
            Pg = V(C.PS[0 + (u % 2)], 0, TG)
            Pv = V(C.PS[2 + (u % 2)], 0, TG)
            for h, P in ((0, Pg), (1, Pv)):
                mm(S, P, [(V(W, k * 256 + h * 128, k * 256 + h * 128 + 128), V(XN, k * XW + HALO + tg * TG, k * XW + HALO + (tg + 1) * TG)) for k in range(NCH)])
            ag, av = accg[u % 2], accv[u % 2]
            tg_th = conv_taps(S, C, Pg, ag, tails[0][tg % 2], tails[0][(tg + 1) % 2] if tg < NTG - 1 else None,
                              [c + i for c in cw], cb + i, 3)
            tv_th = conv_taps(S, C, Pv, av, tails[1][tg % 2], tails[1][(tg + 1) % 2] if tg < NTG - 1 else None,
                              [c + FFN // 128 + i for c in cw], cb + FFN // 128 + i, 3)
            if "c" not in SK:
                for a_, b_ in zip(tg_th, tv_th):
                    a_()
                    b_()
            ge = gel[u % 2] if 'I' not in SK else ag
            if 'D' in SK:
                ge = gel[tg]
            if "g" not in SK:
                if "G" not in SK:
                    act(S, V(ge), V(ag), AF.Gelu_apprx_tanh)
                elif "1" in SK:
                    act(S, V(ge), V(ag), AF.Identity)
                elif "2" in SK:
                    act(S, V(ge), V(C.lnt), AF.Tanh)
                elif "3" in SK:
                    act(S, V(ge), V(C.lnt), AF.Square)
                elif "4" in SK:
                    act(S, V(ge), V(C.lnt), AF.Exp)
                else:
                    act(S, V(ge), V(ag), AF.Tanh)
                if "t" not in SK:
                    tt(S, V(ab, j * T + tg * TG, j * T + (tg + 1) * TG), V(ge), V(av), ALU.mult)

    def down_group(g):
        s = g % 2
        for d in range(NCH):
            for tg in range(NTG):
                u = ucount[0]
                ucount[0] += 1
                P = V(C.PS[6 + (u % 2)], 0, TG)
                mm(S, P, [(V(wdb[s], j * D + d * 128, j * D + d * 128 + 128), V(abuf[s], j * T + tg * TG, j * T + (tg + 1) * TG)) for j in range(G)])
                hv = V(C.H, d * T + tg * TG, d * T + (tg + 1) * TG)
                tt(S, hv, P, hv, ALU.add)

    import os
    lvl = int(os.environ.get("FFN_LVL", "9"))
    if lvl == 0:
        S.release(m0)
        return
    nchk = int(os.environ.get("FFN_NCH", "24"))
    dodown = int(os.environ.get("FFN_DOWN", "1"))
    NG = (nchk + G - 1) // G
    for g in range(NG):
        s = g % 2
        if dodown:
            dma(S, "pool", wdb[s].t[:, :], wdn.t.ap()[g], ("ffn_wd", s), writes=[wdb[s].reg()])
        for j in range(G):
            if g * G + j < nchk:
                up_chunk(g * G + j, j, abuf[s])
        if g >= 1 and dodown:
            down_group(g - 1)
    if dodown:
        down_group(NG - 1)
    S.release(m0)
def rec_phase(S, C, win, wax, wout, hprev=None, hfin=None, exch=None):
    XW = HALO + T
    nh = 3
    m0 = S.mark()
    m1 = S.alloc("rec_m1", NCH * T, BF16)
    m2 = S.alloc("rec_m2", NCH * T, BF16)
    mk_xn = S.mark()
    XN = S.alloc("rec_xn", NCH * XW, BF16)
    wgx = [S.alloc("rec_wgx%d" % i, NCH * 256, BF16) for i in range(2)]
    wab = [S.alloc("rec_wab%d" % i, 256, BF16) for i in range(2)]
    f = lambda nm: S.alloc("rec_" + nm, TG, F32)
    xc, gg, rr, ii, aa, ss, hl, Ac, zz = [f(n) for n in ("xc", "gg", "r", "i", "a", "s", "hl", "Ac", "zz")]
    xcb = S.alloc("rec_xcb", TG, BF16)
    tails = [S.alloc("rec_tl%d" % i, 4, F32) for i in range(2)]
    small = S.alloc("rec_small", 64, F32)
    memset(S, V(zz), 0.0)
    src = lambda c, c0, n: (V(C.HH, c * HALO + c0 + HALO, c * HALO + c0 + HALO + n) if c0 < 0
                            else V(C.H, c * T + c0, c * T + c0 + n))
    dst = lambda c, c0, n: V(XN, c * XW + HALO + c0, c * XW + HALO + c0 + n)
    rmsnorm(S, C, src, dst, C.VEC["rec_norm"], [(-nh, nh)] + [(tg * TG, TG) for tg in range(NTG)])
    ap0 = C.VEC["a_param"]
    act(S, V(small, 0, 8), V(C.vecs, ap0, ap0 + 8), AF.Exp, scale=-1.0)
    act(S, V(small, 0, 8), V(small, 0, 8), AF.Ln, bias=1.0)
    ts(S, V(small, 8, 16), V(small, 0, 8), -8.0, None, ALU.mult)
    ts(S, V(small, 16, 24), V(small, 0, 8), -16.0, None, ALU.mult)
    cw = [C.VEC["rec_cw_%d" % k] for k in range(4)]
    cb = C.VEC["rec_cb"]
    u = 0
    for n in range(NCH):
        W = wgx[n % 2]
        dma(S, "pool", W.t[:, :], win.t.ap()[n], ("rec_wgx", n % 2), writes=[W.reg()])
        WA = wab[n % 2]
        dma(S, "pool", WA.t[:, :], wax.t.ap()[n], ("rec_wab", n % 2), writes=[WA.reg()])
        ph = V(C.PS[4], 0, nh)
        mm(S, ph, [(V(W, k * 256 + 128, k * 256 + 256), V(XN, k * XW + HALO - nh, k * XW + HALO)) for k in range(NCH)])
        cp(S, V(tails[0], 0, nh), ph)
        for tg in range(NTG):
            Pg = V(C.PS[0 + (u % 2)], 0, TG)
            Px = V(C.PS[2 + (u % 2)], 0, TG)
            u += 1
            for h, P in ((0, Pg), (1, Px)):
                mm(S, P, [(V(W, k * 256 + h * 128, k * 256 + h * 128 + 128), V(XN, k * XW + HALO + tg * TG, k * XW + HALO + (tg + 1) * TG)) for k in range(NCH)])
            for th in conv_taps(S, C, Px, xc, tails[tg % 2], tails[(tg + 1) % 2] if tg < NTG - 1 else None,
                                [c + n for c in cw], cb + n, 4):
                th()
            act(S, V(gg), Pg, AF.Gelu_apprx_tanh)
            cp(S, V(xcb), V(xc))
            Pr = V(C.PS[4], 0, TG)
            Pi = V(C.PS[5], 0, TG)
            mm(S, Pr, [(V(WA, 0, 128), V(xcb))])
            mm(S, Pi, [(V(WA, 128, 256), V(xcb))])
            act(S, V(rr), Pr, AF.Sigmoid, bias=V(C.vecs, C.VEC["ga_b"] + n, C.VEC["ga_b"] + n + 1))
            act(S, V(ii), Pi, AF.Sigmoid, bias=V(C.vecs, C.VEC["gx_b"] + n, C.VEC["gx_b"] + n + 1))
            act(S, V(aa), V(rr), AF.Exp, scale=V(small, 8 + n, 9 + n))
            act(S, V(ss), V(rr), AF.Exp, scale=V(small, 16 + n, 17 + n))
            act(S, V(ss), V(ss), AF.Sqrt, bias=1.0, scale=-1.0)
            tt(S, V(ii), V(ii), V(xc), ALU.mult)
            tt(S, V(ii), V(ii), V(ss), ALU.mult)
            if tg > 0:
                cp(S, V(small, 24, 25), V(hl, TG - 1, TG))
                cp(S, V(small, 25, 26), V(Ac, TG - 1, TG))
            ih = V(small, 24, 25) if tg > 0 else 0.0
            ia = V(small, 25, 26) if tg > 0 else 1.0
            S.op("dve", lambda e, ih=ih: e.tensor_tensor_scan(V(hl).ap(), V(aa).ap(), V(ii).ap(), _sc(ih), ALU.mult, ALU.add),
                 reads=_rd(V(aa), V(ii), ih), writes=[V(hl).reg()])
            S.op("dve", lambda e, ia=ia: e.tensor_tensor_scan(V(Ac).ap(), V(aa).ap(), V(zz).ap(), _sc(ia), ALU.mult, ALU.add),
                 reads=_rd(V(aa), V(zz), ia), writes=[V(Ac).reg()])
            tt(S, V(m1, n * T + tg * TG, n * T + (tg + 1) * TG), V(hl), V(gg), ALU.mult)
            tt(S, V(m2, n * T + tg * TG, n * T + (tg + 1) * TG), V(Ac), V(gg), ALU.mult)
        cp(S, V(C.hfin, n, n + 1), V(hl, TG - 1, TG))
    S.release(mk_xn)
    if exch is not None:
        exch()
    wo = S.alloc("rec_wo", NCH * D, BF16)
    wo2 = S.alloc("rec_wo2", NCH * D, BF16)
    for n in range(NCH):
        dma(S, "pool", wo.t[:, n * D:(n + 1) * D], wout.t.ap()[n], ("rec_wo", n), writes=[wo.reg(n * D, (n + 1) * D)])
        ts(S, V(wo2, n * D, (n + 1) * D), V(wo, n * D, (n + 1) * D), V(C.hprev, n, n + 1), None, ALU.mult)
    for d in range(NCH):
        for tg in range(NTG):
            P = V(C.PS[6 + (u % 2)], 0, TG)
            u += 1
            pairs = [(V(wo, n * D + d * 128, n * D + d * 128 + 128), V(m1, n * T + tg * TG, n * T + (tg + 1) * TG)) for n in range(NCH)]
            pairs += [(V(wo2, n * D + d * 128, n * D + d * 128 + 128), V(m2, n * T + tg * TG, n * T + (tg + 1) * TG)) for n in range(NCH)]
            mm(S, P, pairs)
            hv = V(C.H, d * T + tg * TG, d * T + (tg + 1) * TG)
            tt(S, hv, P, hv, ALU.add)
    S.release(m0)
LAM_INIT0 = 0.8 - 0.6 * math.exp(-0.3 * 0)
TK = 2 * T


def attn_consts(S, C, cst_d):
    C.cst = S.alloc("acst", ACST_N, F32)
    dma(S, "sp", C.cst.t[:, :], cst_d.t.ap(), "acst", writes=[C.cst.reg()])
    C.ident = S.alloc("ident", 128, BF16)
    C.bd64 = S.alloc("bd64", 128, BF16)
    C.ones_f = S.alloc("ones_f", 128, F32)
    cp(S, V(C.ident), V(C.cst, ACST["ident"], ACST["ident"] + 128))
    cp(S, V(C.bd64), V(C.cst, ACST["bd64"], ACST["bd64"] + 128))
    memset(S, V(C.ones_f), 1.0)
    sm = S.alloc("asmall", 16, F32)
    C.asm = sm
    vv = lambda nm: V(C.vecs, C.VEC[nm], C.VEC[nm] + 1)
    tt(S, V(sm, 0, 1), vv("lq1"), vv("lk1"), ALU.mult)
    tt(S, V(sm, 1, 2), vv("lq2"), vv("lk2"), ALU.mult)
    ps = V(C.PS[4], 0, 2)
    mm(S, ps, [(V(C.ones_f), V(sm, 0, 2))])
    act(S, V(sm, 2, 4), ps, AF.Exp)
    tt(S, V(sm, 4, 5), V(sm, 3, 4), V(sm, 2, 3), ALU.subtract)
    ts(S, V(sm, 5, 6), V(sm, 4, 5), -LAM_INIT0, None, ALU.add)
    ts(S, V(sm, 6, 7), vv("bsub"), 1.0 - LAM_INIT0, None, ALU.mult)
    C.nlam = V(sm, 5, 6)
    C.gsub = V(sm, 6, 7)


def qk_norm_store(S, C, P, n, gain, dst, ubuf):
    sq = V(C.sqb[ubuf % 3], 0, n)
    act(S, sq, P, AF.Square)
    st = V(C.PS[4 + (ubuf % 2)], 0, n)
    mm(S, st, [(V(C.bd64), sq)])
    lnt = V(C.alnt[ubuf % 2], 0, n)
    act(S, lnt, st, AF.Ln, bias=EPS, scale=1.0 / 64)
    act(S, lnt, lnt, AF.Exp, scale=-0.5)
    stt(S, dst, P, gain, lnt, ALU.mult, ALU.mult)


def attn_phase(S, C, xo_d, xh_d, win_d, wo_d, hout_d=None):
    m0 = S.mark()
    if hout_d is None:
        XN = S.alloc_at(C.H.off, NCH * T, BF16)
        XNH = S.alloc_at(C.H.off + NCH * T * 2, NCH * T, BF16)
    else:
        XN = S.alloc("at_xn", NCH * T, BF16)
        XNH = S.alloc("at_xnh", NCH * T, BF16)
    Y = S.alloc("at_y", NCH * T, BF16)
    C.alnt = [S.alloc("at_lnt%d" % i, TG, F32) for i in range(2)]
    m1 = S.mark()
    stage = S.alloc("at_stage", NCH * TG, F32)
    for (xd, dstb, nm) in ((xh_d, XNH, "h"), (xo_d, XN, "o")):
        for tg in range(NTG):
            for c in range(NCH):
                dma(S, "sp", stage.t[:, c * TG:(c + 1) * TG], xd.t.ap()[:, c * T + tg * TG:c * T + (tg + 1) * TG], ("at_stage", c),
                    writes=[stage.reg(c * TG, (c + 1) * TG)])
            rmsnorm(S, C, lambda c, c0, n: V(stage, c * TG, c * TG + n), lambda c, c0, n, dstb=dstb, tg=tg: V(dstb, c * T + tg * TG, c * T + tg * TG + n),
                    C.VEC["attn_norm"], [(0, TG)])
    S.release(m1)
    wqkv = [S.alloc("at_w%d" % i, NCH * 384, BF16) for i in range(2)]
    KT = S.alloc("at_kt", TK, BF16)
    QT = S.alloc("at_qt", T, BF16)
    VT = S.alloc("at_vt", TK, BF16)
    VA = S.alloc("at_va", 32 * 256, BF16)
    OD = [S.alloc("at_od%d" % i, T, F32) for i in range(2)]
    Ssb = [S.alloc("at_ssb%d" % i, TG, F32) for i in range(4)]
    Pt = [S.alloc("at_pt%d" % i, TG, BF16) for i in range(4)]
    tmp = [S.alloc("at_tmp%d" % i, TG, F32) for i in range(3)]
    b2 = [S.alloc("at_b2%d" % i, 256, F32) for i in range(2)]
    PSb = [C.PS[6].t[:, :].bitcast(BF16), C.PS[7].t[:, :].bitcast(BF16)]
    vv = lambda nm: V(C.vecs, C.VEC[nm], C.VEC[nm] + 1)
    padneg = vv("padneg")
    uc = [0]

    def nxt():
        uc[0] += 1
        return uc[0]

    def project(g):
        W = wqkv[g % 2]
        dma(S, "pool", W.t[:, :], win_d.t.ap()[g], ("at_w", g % 2), writes=[W.reg()])
        gq, gk = (vv("aq"), vv("ak")) if g < 4 else (vv("bq"), vv("bk"))
        for tgk in range(2 * NTG):
            srcb, t0 = (XNH, tgk * TG) if tgk < NTG else (XN, (tgk - NTG) * TG)
            xs = lambda k: V(srcb, k * T + t0, k * T + t0 + TG)
            u = nxt()
            P = V(C.PS[u % 4], 0, TG)
            mm(S, P, [(V(W, k * 384 + 128, k * 384 + 256), xs(k)) for k in range(NCH)])
            qk_norm_store(S, C, P, TG, gk, V(KT, tgk * TG, (tgk + 1) * TG), u)
            u = nxt()
            P = V(C.PS[u % 4], 0, TG)
            mm(S, P, [(V(W, k * 384 + 256, k * 384 + 384), xs(k)) for k in range(NCH)])
            cp(S, V(VT, tgk * TG, (tgk + 1) * TG), P, eng="act")
            if tgk >= NTG:
                u = nxt()
                P = V(C.PS[u % 4], 0, TG)
                mm(S, P, [(V(W, k * 384, k * 384 + 128), xs(k)) for k in range(NCH)])
                qk_norm_store(S, C, P, TG, gq, V(QT, t0, t0 + TG), u)

    def v_blocks(d, jmin, aug):
        nb = 32 // d
        for r in range(d):
            for j in range(jmin, nb):
                slot = r * nb + j
                u = nxt()
                pb = PSb[u % 2]
                c0 = 128 * j * d + r
                src = V(VT, c0, c0 + 127 * d + 1, step=d)
                outp = pb[:, 0:128]
                S.op("pe", lambda e, outp=outp, src=src: e.transpose(outp, src.ap(), C.ident.t[:, :]),
                     reads=[src.reg(), C.ident.reg()], writes=[C.PS[6 + (u % 2)].reg()])
                if aug:
                    dv = VA.t[:, slot * 256:(slot + 1) * 256].rearrange("p (h c) -> p h c", h=2)[:, :, 0:64]
                    sv = outp.rearrange("p (h c) -> p h c", h=2)
                    S.op("dve", lambda e, dv=dv, sv=sv: e.tensor_copy(dv, sv), reads=[C.PS[6 + (u % 2)].reg()],
                         writes=[VA.reg(slot * 256, (slot + 1) * 256)])
                else:
                    dv = VA.t[:, slot * 128:(slot + 1) * 128]
                    S.op("dve", lambda e, dv=dv, outp=outp: e.tensor_copy(dv, outp), reads=[C.PS[6 + (u % 2)].reg()],
                         writes=[VA.reg(slot * 128, (slot + 1) * 128)])

    def dilated_pair(g):
        ones3 = VA.t[:, :].rearrange("p (s h c) -> p s h c", h=2, c=128)[:, :, :, 64:128]
        S.op("pool", lambda e: e.memset(ones3, 1.0), writes=[VA.reg()])
        for b, (w_, d) in enumerate(DIL):
            nb = 32 // d
            jmin = 16 // d - 1
            v_blocks(d, jmin, True)
            for ph in range(2):
                hd = 2 * g + ph
                sl8 = -8.0 * SLOPES[hd] * d
                stt(S, V(b2[0]), V(C.cst, ACST["ddist"], ACST["ddist"] + 256), sl8, V(C.cst, ACST["dmask"], ACST["dmask"] + 256), ALU.mult, ALU.add)
                ts(S, V(b2[1], 0, 128), V(b2[0], 0, 128), padneg, None, ALU.add)
                cp(S, V(b2[1], 128, 256), V(b2[0], 128, 256))
                p0, p1 = ph * 64, ph * 64 + 64
                units = [(r, j) for r in range(d) for j in range(16 // d, nb)]
                st = {}

                def stage_a(i, ph=ph, p0=p0, p1=p1, d=d, nb=nb, units=units, st=st):
                    r, j = units[i]
                    u = nxt()
                    Sp = C.PS[u % 4]
                    q0 = 128 * j * d + r - T
                    qv = V(QT, q0, q0 + 127 * d + 1, p0, p1, step=d)
                    for i2, jj in enumerate((j - 1, j)):
                        k0 = 128 * jj * d + r
                        kv = V(KT, k0, k0 + 127 * d + 1, p0, p1, step=d)
                        mm(S, V(Sp, i2 * 128, i2 * 128 + 128), [(kv, qv)])
                    hist = (j - 1) < 16 // d
                    sb = Ssb[u % 4]
                    tt(S, V(sb, 0, 256), V(Sp, 0, 256), V(b2[1 if hist else 0]), ALU.add)
                    pt = Pt[u % 4]
                    act(S, V(pt, 0, 256), V(sb, 0, 256), AF.Exp, scale=0.125)
                    st[i] = (u, q0, pt)

                def stage_b(i, ph=ph, d=d, nb=nb, units=units, st=st, b=b):
                    r, j = units[i]
                    u, q0, pt = st[i]
                    Op = C.PS[4 + (u % 2)]
                    pairs = []
                    for i2, jj in enumerate((j - 1, j)):
                        slot = r * nb + jj
                        pairs.append((V(VA, slot * 256 + ph * 128, slot * 256 + ph * 128 + 128), V(pt, i2 * 128, i2 * 128 + 128)))
                    mm(S, V(Op, 0, 128), pairs)
                    ov = V(OD[ph], q0, q0 + 127 * d + 1, step=d)
                    if b == 0:
                        cp(S, ov, V(Op, 0, 128))
                    else:
                        tt(S, ov, V(Op, 0, 128), ov, ALU.add)
                LA = 2
                for i in range(len(units) + LA):
                    if i < len(units):
                        stage_a(i)
                    if i >= LA:
                        stage_b(i - LA)
        for ph in range(2):
            for tg in range(NTG):
                rc = V(tmp[tg % 2], 0, TG, 0, 64)
                S.op("dve", lambda e, rc=rc, ph=ph, tg=tg: e.reciprocal(rc.ap(), OD[ph].t[64:128, tg * TG:(tg + 1) * TG]),
                     reads=[OD[ph].reg(tg * TG, (tg + 1) * TG, 64, 128)], writes=[rc.reg()])
                tt(S, V(Y, g * T + tg * TG, g * T + (tg + 1) * TG, ph * 64, ph * 64 + 64), V(OD[ph], tg * TG, (tg + 1) * TG, 0, 64), rc, ALU.mult)

    def diff_head(h):
        g = 4 + h
        v_blocks(1, 0, False)
        sl = SLOPES[8 + h]
        for G in range(NTG):
            q0 = G * TG
            nkb = 16 + 4 * G + 4
            O = [C.PS[4], C.PS[5]]
            Dn = [C.PS[6], C.PS[7]]
            st = {}
            acc0 = tmp[0]

            def stage_a(kb, G=G, q0=q0, st=st):
                jj = kb - (16 + 4 * G)
                bcol = ACST["dbias"] + (h * 2 + (1 if kb < 16 else 0)) * 36 + (kb - 4 * G + 12)
                bias = V(C.cst, bcol, bcol + 1)
                us = [nxt(), nxt()]
                Sps = [V(C.PS[u % 4], 0, TG) for u in us]
                for c in range(2):
                    mm(S, Sps[c], [(V(KT, kb * 128, kb * 128 + 128, c * 64, c * 64 + 64), V(QT, q0, q0 + TG, c * 64, c * 64 + 64))])
                pts = []
                for c in range(2):
                    u, Sp = us[c], Sps[c]
                    pt = V(Pt[u % 4])
                    if jj < 0:
                        act(S, pt, Sp, AF.Exp, bias=bias, scale=0.125)
                    else:
                        sb = Ssb[u % 4]
                        if jj > 0:
                            memset(S, V(sb, 0, jj * 128), 8 * NEG, eng="pool")
                        tt(S, V(sb, jj * 128, jj * 128 + 128), V(Sp.buf, jj * 128, jj * 128 + 128), V(C.cst, ACST["tri8"], ACST["tri8"] + 128), ALU.add)
                        if jj < 3:
                            cp(S, V(sb, jj * 128 + 128, TG), V(Sp.buf, jj * 128 + 128, TG))
                        act(S, pt, V(sb), AF.Exp, bias=bias, scale=0.125)
                    pts.append(pt)
                if kb == 0:
                    cp(S, V(acc0), pts[0])
                else:
                    tt(S, V(acc0), V(acc0), pts[0], ALU.add)
                st[kb] = pts

            def stage_b(kb, st=st, nkb=nkb):
                pts = st[kb]
                for c in range(2):
                    mm(S, V(O[c], 0, TG), [(V(VA, kb * 128, kb * 128 + 128), pts[c])], start=(kb == 0), stop=(kb == nkb - 1))
                mm(S, V(Dn[1], 0, TG), [(V(C.ones_bf), pts[1])], start=(kb == 0), stop=(kb == nkb - 1))
            LA = 1
            for i in range(nkb + LA):
                if i < nkb:
                    stage_a(i)
                if i >= LA:
                    stage_b(i - LA)
            mm(S, V(Dn[0], 0, TG), [(V(C.ones_f), V(acc0))])
            t0_, t1_, t2_ = V(tmp[0]), V(tmp[1]), V(tmp[2])
            S.op("dve", lambda e: e.reciprocal(t0_.ap(), Dn[0].t[:, 0:TG]), reads=[Dn[0].reg()], writes=[t0_.reg()])
            tt(S, t0_, V(O[0], 0, TG), t0_, ALU.mult)
            S.op("dve", lambda e: e.reciprocal(t1_.ap(), Dn[1].t[:, 0:TG]), reads=[Dn[1].reg()], writes=[t1_.reg()])
            tt(S, t1_, V(O[1], 0, TG), t1_, ALU.mult)
            stt(S, t0_, t1_, C.nlam, t0_, ALU.mult, ALU.add)
            sq = V(C.sqb[0], 0, TG)
            act(S, sq, t0_, AF.Square)
            st = V(C.PS[0], 0, TG)
            mm(S, st, [(V(C.ones_bf), sq)])
            act(S, t2_, st, AF.Ln, bias=EPS, scale=1.0 / 128)
            act(S, t2_, t2_, AF.Exp, scale=-0.5)
            stt(S, V(Y, g * T + q0, g * T + q0 + TG), t0_, C.gsub, t2_, ALU.mult, ALU.mult)

    for g in range(8):
        project(g)
        if g < 4:
            dilated_pair(g)
        else:
            diff_head(g - 4)
    if getattr(C, "ydbg", None) is not None:
        dma(S, "sp", C.ydbg.t.ap(), Y.t[:, :], "ydbg", reads=[Y.reg()], writes=[C.ydbg.reg()])
    wo = S.alloc("at_wo", NCH * D, BF16) if False else None
    S.release(m0)
    m2 = S.mark()
    Yk = Y
    S.sp_ = Y.off + NCH * T * 2
    wo = S.alloc("at_wo", NCH * D, BF16)
    xr_ = [S.alloc("at_xr%d" % i, TG, F32) for i in range(2)]
    rs_ = [S.alloc("at_rs%d" % i, TG, F32) for i in range(2)]
    for n in range(NCH):
        dma(S, "pool", wo.t[:, n * D:(n + 1) * D], wo_d.t.ap()[n], ("at_wo", n), writes=[wo.reg(n * D, (n + 1) * D)])
    for dch in range(NCH):
        for tg in range(NTG):
            u = nxt()
            P = V(C.PS[u % 4], 0, TG)
            mm(S, P, [(V(wo, k * D + dch * 128, k * D + dch * 128 + 128), V(Y, k * T + tg * TG, k * T + (tg + 1) * TG)) for k in range(NCH)])
            xr = xr_[u % 2]
            rs = rs_[u % 2]
            dma(S, "sp", xr.t[:, :], xo_d.t.ap()[:, dch * T + tg * TG:dch * T + (tg + 1) * TG], ("at_xr", u % 2), writes=[xr.reg()])
            if hout_d is None:
                tt(S, V(C.H, dch * T + tg * TG, dch * T + (tg + 1) * TG), P, V(xr), ALU.add)
                continue
            tt(S, V(rs), P, V(xr), ALU.add)
            dma(S, "sp", hout_d.t.ap()[:, dch * T + tg * TG:dch * T + (tg + 1) * TG], rs.t[:, :], ("hout", u % 2),
                reads=[rs.reg()], writes=[hout_d.reg(dch * T + tg * TG, dch * T + (tg + 1) * TG)])
    S.release(m0)
ARENA_BYTES = 204 * 1024


class Ctx:
    pass


def make_ctx(S, with_H=True):
    C = Ctx()
    C.VEC = VEC
    C.PS = [S.psum("ps%d" % i) for i in range(8)]
    C.vecs = S.alloc("vecs", NV, F32)
    C.ones_bf = S.alloc("ones_bf", 128, BF16)
    C.sqb = [S.alloc("sqb%d" % i, TG, BF16) for i in range(3)]
    C.lnt = S.alloc("lnt", TG, F32)
    if with_H:
        C.H = S.alloc("H", NCH * T, F32)
    C.HH = S.alloc("HH", NCH * HALO, F32)
    C.hprev = S.alloc("hprev", 8, F32)
    C.hfin = S.alloc("hfin", 8, F32)
    return C


def load_common(S, C, vecs_d):
    dma(S, "sp", C.vecs.t[:, :], vecs_d.t.ap(), "vecs", writes=[C.vecs.reg()])
    memset(S, V(C.ones_bf), 1.0)


def load_H(S, C, hin_d, halo_d):
    for c in range(NCH):
        dma(S, "sp", C.H.t[:, c * T:(c + 1) * T], hin_d.t.ap()[:, c * T:(c + 1) * T], ("hin", c), writes=[C.H.reg(c * T, (c + 1) * T)])
    dma(S, "sp", C.HH.t[:, :], halo_d.t.ap(), "halo", writes=[C.HH.reg()])


def store_H(S, C, hout_d):
    for c in range(NCH):
        dma(S, "sp", hout_d.t.ap()[:, c * T:(c + 1) * T], C.H.t[:, c * T:(c + 1) * T], "hout",
            reads=[C.H.reg(c * T, (c + 1) * T)], writes=[hout_d.reg(c * T, (c + 1) * T)])


def build_ffn_program(L):
    nc = bass.Bass("TRN2", target_bir_lowering=False)
    S = Sched(nc, ARENA_BYTES)
    C = make_ctx(S)
    vecs_d = S.dram("vecs", [128, NV], F32, "ExternalInput")
    hin_d = S.dram("hin", [128, NCH * T], F32, "ExternalInput")
    halo_d = S.dram("halo", [128, NCH * HALO], F32, "ExternalInput")
    wup_d = S.dram("wup", [24, 128, NCH * 256], F32, "ExternalInput")
    wdn_d = S.dram("wdn", [6, 128, 4 * D], F32, "ExternalInput")
    hout_d = S.dram("hout", [128, NCH * T], F32, "ExternalOutput")
    load_common(S, C, vecs_d)
    load_H(S, C, hin_d, halo_d)
    ffn_phase(S, C, L, wup_d, wdn_d)
    store_H(S, C, hout_d)
    S.emit(final_waits=["hout"])
    return nc, S


def build_rec_program():
    nc = bass.Bass("TRN2", target_bir_lowering=False)
    S = Sched(nc, ARENA_BYTES)
    C = make_ctx(S)
    vecs_d = S.dram("vecs", [128, NV], F32, "ExternalInput")
    hin_d = S.dram("hin", [128, NCH * T], F32, "ExternalInput")
    halo_d = S.dram("halo", [128, NCH * HALO], F32, "ExternalInput")
    hprev_d = S.dram("hprev", [128, 8], F32, "ExternalInput")
    win_d = S.dram("win", [8, 128, NCH * 256], F32, "ExternalInput")
    wax_d = S.dram("wax", [8, 128, 256], F32, "ExternalInput")
    wout_d = S.dram("wout", [8, 128, D], F32, "ExternalInput")
    hout_d = S.dram("hout", [128, NCH * T], F32, "ExternalOutput")
    hfin_d = S.dram("hfin", [128, 8], F32, "ExternalOutput")
    load_common(S, C, vecs_d)
    load_H(S, C, hin_d, halo_d)
    dma(S, "sp", C.hprev.t[:, :], hprev_d.t.ap(), "hprev", writes=[C.hprev.reg()])
    rec_phase(S, C, win_d, wax_d, wout_d, None, None)
    store_H(S, C, hout_d)
    dma(S, "sp", hfin_d.t.ap(), C.hfin.t[:, :], "hfin", reads=[C.hfin.reg()], writes=[hfin_d.reg()])
    S.emit(final_waits=["hout", "hfin"])
    return nc, S


def build_attn_program():
    nc = bass.Bass("TRN2", target_bir_lowering=False)
    S = Sched(nc, ARENA_BYTES)
    C = make_ctx(S, with_H=False)
    vecs_d = S.dram("vecs", [128, NV], F32, "ExternalInput")
    cst_d = S.dram("acst", [128, ACST_N], F32, "ExternalInput")
    xo_d = S.dram("xo", [128, NCH * T], F32, "ExternalInput")
    xh_d = S.dram("xh", [128, NCH * T], F32, "ExternalInput")
    win_d = S.dram("win", [8, 128, NCH * 384], F32, "ExternalInput")
    wo_d = S.dram("wo", [8, 128, D], F32, "ExternalInput")
    hout_d = S.dram("hout", [128, NCH * T], F32, "ExternalOutput")
    load_common(S, C, vecs_d)
    attn_consts(S, C, cst_d)
    fw = [("hout", 0), ("hout", 1)]
    attn_phase(S, C, xo_d, xh_d, win_d, wo_d, hout_d)
    S.emit(final_waits=fw)
    return nc, S


def exchange(S, C, idx, payload, dst, w):
    snd = S.dram("xsnd%d" % idx, [128, w], F32, "Internal")
    gat = S.dram("xgat%d" % idx, [8 * 128, w], F32, "Internal")
    dma(S, "sp", snd.t.ap(), payload.ap(), ("xs", idx), reads=[payload.reg()], writes=[snd.reg()])
    S.op("pool", lambda e: e.collective_compute("AllGather", ALU.bypass, replica_groups=[list(range(8))],
                                                ins=[snd.t.ap().opt()], outs=[gat.t.ap().opt()]),
         reads=[snd.reg()], writes=[gat.reg()], dma_key=("xc", idx), dma_inc=1)
    S.op("pool", lambda e: e.memset(C.ccscr.t[:, idx:idx + 1], 0.0), reads=[gat.reg()], writes=[C.ccscr.reg(idx, idx + 1)])
    m = S.mark()
    G = S.alloc("xg%d" % idx, 8 * w, F32)
    dma(S, "sp", G.t[:, :].rearrange("p (r w) -> p r w", r=8), gat.t.ap().rearrange("(r p) w -> p r w", p=128), ("xg", idx),
        reads=[gat.reg()], writes=[G.reg()])
    s0 = C.VEC["sel"]
    ts(S, dst, V(G, 0, w), V(C.vecs, s0, s0 + 1), None, ALU.mult)
    for r in range(1, 8):
        stt(S, dst, V(G, r * w, (r + 1) * w), V(C.vecs, s0 + r, s0 + r + 1), dst, ALU.mult, ALU.add)
    S.release(m)


def halo_exchange(S, C, idx):
    m = S.mark()
    pay = S.alloc("xpay%d" % idx, NCH * HALO, F32)
    src = C.H.t[:, :].rearrange("p (c t) -> p c t", c=NCH)[:, :, T - HALO:T]
    dstv = pay.t[:, :].rearrange("p (c h) -> p c h", c=NCH)
    S.op("dve", lambda e: e.tensor_copy(dstv, src), reads=[C.H.reg()], writes=[pay.reg()])
    exchange(S, C, idx, V(pay), V(C.HH), NCH * HALO)
    S.release(m)


def build_fused_program():
    nc = bass.Bass("TRN2", target_bir_lowering=False)
    S = Sched(nc, ARENA_BYTES)
    C = make_ctx(S, with_H=True)
    dr = lambda n, shp: S.dram(n, shp, F32, "ExternalInput")
    vecs_d = dr("vecs", [128, NV])
    cst_d = dr("acst", [128, ACST_N])
    xo_d = dr("xo", [128, NCH * T])
    xh_d = dr("xh", [128, NCH * T])
    awin_d = dr("awin", [8, 128, NCH * 384])
    awo_d = dr("awo", [8, 128, D])
    wup_d = [dr("wup%d" % L, [24, 128, NCH * 256]) for L in range(2)]
    wdn_d = [dr("wdn%d" % L, [6, 128, 4 * D]) for L in range(2)]
    rwin_d = dr("rwin", [8, 128, NCH * 256])
    rwax_d = dr("rwax", [8, 128, 256])
    rwo_d = dr("rwo", [8, 128, D])
    hout_d = S.dram("hout", [128, NCH * T], F32, "ExternalOutput")
    C.ccscr = S.alloc("ccscr", 16, F32)
    load_common(S, C, vecs_d)
    m = S.mark()
    attn_consts(S, C, cst_d)
    attn_phase(S, C, xo_d, xh_d, awin_d, awo_d, None)
    S.release(m)
    halo_exchange(S, C, 0)
    ffn_phase(S, C, 0, wup_d[0], wdn_d[0])
    halo_exchange(S, C, 1)
    rec_phase(S, C, rwin_d, rwax_d, rwo_d, exch=lambda: exchange(S, C, 2, V(C.hfin), V(C.hprev), 8))
    halo_exchange(S, C, 3)
    ffn_phase(S, C, 1, wup_d[1], wdn_d[1])
    store_H(S, C, hout_d)
    S.emit(final_waits=["hout"])
    return nc, S


_FUSED = []


def kernel(**inputs):
    inp = {k: np.asarray(v) for k, v in inputs.items()}
    x = inp["x"]
    cores = list(range(8))
    if not _FUSED:
        _FUSED.append(build_fused_program()[0])
    nc = _FUSED[0]
    shared = {
        "awin": lay_attn_win(inp["attn_w_in"][0]),
        "awo": np.ascontiguousarray(inp["attn_w_out"][0].reshape(8, 128, D)),
        "wup0": lay_wup(inp["ffn_w_up"][0]), "wdn0": lay_wdn(inp["ffn_w_down"][0]),
        "wup1": lay_wup(inp["ffn_w_up"][1]), "wdn1": lay_wdn(inp["ffn_w_down"][1]),
        "rwin": lay_win(inp["rec_w_in"][0]),
        "rwax": lay_wax(inp["rec_gate_a_w"][0], inp["rec_gate_x_w"][0]),
        "rwo": np.ascontiguousarray(inp["rec_w_out"][0].reshape(8, 128, D)),
    }
    acst = [build_attn_consts(0), build_attn_consts(1)]
    in_maps = []
    for c in cores:
        b, half = c // 2, c % 2
        mp = dict(shared)
        mp["vecs"] = build_vecs(inp, half, c)
        mp["acst"] = acst[half]
        mp["xo"] = lay_T(x[b, half * T:(half + 1) * T])
        mp["xh"] = lay_T(x[b, 0:T]) if half == 1 else np.zeros((128, NCH * T), np.float32)
        in_maps.append(mp)
    res = run_bass_kernel_spmd(nc, in_maps, core_ids=cores).results
    out = np.zeros_like(x)
    for c in cores:
        out[c // 2, (c % 2) * T:(c % 2 + 1) * T] = unlay_T(res[c]["hout"], T)
    return out
```
